# Optimizing a Trainium2 kernel written in Bass

```python
import math
import jax
import jax.numpy as jnp
from jax import lax
import numpy as np

D_MODEL = 2048
BATCH = 1
SEQ = 16384
DEPTH = 2

GRID_W = 64
CTX_LEN = 256

S5_WIDTH = 1024
S5_GROUP = 16
S5_GROUPS = S5_WIDTH // S5_GROUP
S5_STATE = 64

SSD_WIDTH = 2048
SSD_HEAD_DIM = 64
SSD_HEADS = SSD_WIDTH // SSD_HEAD_DIM
SSD_GROUPS = 4
SSD_STATE = 128
SSD_CONV = 3
SSD_CHUNK = 128
SSD_CONV_CH = SSD_WIDTH + 2 * SSD_GROUPS * SSD_STATE

D_FF = 5632
FFN_CONV = 3

COL_XBC = S5_WIDTH
COL_DT = COL_XBC + SSD_CONV_CH
COL_Z = COL_DT + 2 * SSD_HEADS
COL_GATES = COL_Z + SSD_WIDTH
PROJ_COLS = COL_GATES + 2 * D_MODEL

DEEPNORM_ALPHA = (2 * DEPTH) ** 0.25
DEEPNORM_BETA = (8 * DEPTH) ** -0.25
LN_EPS = 1e-5
RMS_EPS = 1e-5
F32 = jnp.float32

kernel_name = 'hybrid_s5_ssd_convffn_diffusion_block'


def layer_norm(x, g, b):
    xf = x.astype(F32)
    mu = jnp.mean(xf, axis=-1, keepdims=True)
    var = jnp.mean(jnp.square(xf - mu), axis=-1, keepdims=True)
    return ((xf - mu) * lax.rsqrt(var + LN_EPS) * g.astype(F32) + b.astype(F32)).astype(x.dtype)


def modulate(h, shift, scale):
    return h * (1.0 + scale) + shift


def adaln(cvec, w, b):
    return jax.nn.silu(cvec) @ w + b


def _ident(a):
    return a


def _rev(a):
    return jnp.flip(a, axis=1)


def dwconv_seq(x, w, b):
    k, ch = w.shape
    y = lax.conv_general_dilated(x, w[:, None, :], window_strides=(1,), padding=[(k // 2, k // 2)],
                                 dimension_numbers=('NWC', 'WIO', 'NWC'), feature_group_count=ch)
    return y + b


def dwconv_grid(x, w, b):
    bsz, n, ch = x.shape
    rows = n // GRID_W
    k = w.shape[0]
    img = x.reshape(bsz, rows, GRID_W, ch)
    y = lax.conv_general_dilated(img, w[:, :, None, :], window_strides=(1, 1),
                                 padding=[(k // 2, k // 2), (k // 2, k // 2)],
                                 dimension_numbers=('NHWC', 'HWIO', 'NHWC'), feature_group_count=ch)
    return (y + b).reshape(bsz, n, ch)


def s5_discretise(lam_re, lam_im, log_step, b_re, b_im):
    lam_re = jnp.minimum(lam_re.astype(F32), -1e-4)
    lam_im = lam_im.astype(F32)
    step = jnp.exp(log_step.astype(F32))[:, None]
    mag = jnp.exp(lam_re * step)
    ab_re = mag * jnp.cos(lam_im * step)
    ab_im = mag * jnp.sin(lam_im * step)
    den = jnp.square(lam_re) + jnp.square(lam_im)
    q_re = ((ab_re - 1.0) * lam_re + ab_im * lam_im) / den
    q_im = (ab_im * lam_re - (ab_re - 1.0) * lam_im) / den
    b_re = b_re.astype(F32)
    b_im = b_im.astype(F32)
    bb_re = q_re[..., None] * b_re - q_im[..., None] * b_im
    bb_im = q_re[..., None] * b_im + q_im[..., None] * b_re
    return ab_re, ab_im, bb_re, bb_im


def _complex_affine_combine(e1, e2):
    a1r, a1i, b1r, b1i = e1
    a2r, a2i, b2r, b2i = e2
    return (a2r * a1r - a2i * a1i,
            a2r * a1i + a2i * a1r,
            a2r * b1r - a2i * b1i + b2r,
            a2r * b1i + a2i * b1r + b2i)


def s5_states(u, disc, s0=None):
    ab_re, ab_im, bb_re, bb_im = disc
    bu_re = jnp.einsum('blgj,gpj->blgp', u, bb_re)
    bu_im = jnp.einsum('blgj,gpj->blgp', u, bb_im)
    if s0 is not None:
        s0_re, s0_im = s0
        bu_re = bu_re.at[:, 0].add(ab_re * s0_re - ab_im * s0_im)
        bu_im = bu_im.at[:, 0].add(ab_re * s0_im + ab_im * s0_re)
    a_re = jnp.broadcast_to(ab_re, bu_re.shape)
    a_im = jnp.broadcast_to(ab_im, bu_im.shape)
    _, _, s_re, s_im = lax.associative_scan(_complex_affine_combine, (a_re, a_im, bu_re, bu_im), axis=1)
    return s_re, s_im


def s5_readout(s_re, s_im, c_re, c_im):
    return jnp.einsum('blgp,gjp->blgj', s_re, c_re.astype(F32)) - jnp.einsum('blgp,gjp->blgj', s_im, c_im.astype(F32))


def s5_glu(y, u, d, w_glu):
    bsz, n = y.shape[:2]
    y = (y + u * d.astype(F32).reshape(S5_GROUPS, S5_GROUP)).reshape(bsz, n, S5_WIDTH)
    a = jax.nn.gelu(y) @ w_glu
    return a[..., :S5_WIDTH] * jax.nn.sigmoid(a[..., S5_WIDTH:])


def s5_branch(u_lat, u_ctx, lam_re, lam_im, log_step, b_re, b_im, c_re, c_im, d, w_glu, ctx_out):
    bsz = u_lat.shape[0]
    ul = u_lat.astype(F32).reshape(bsz, u_lat.shape[1], S5_GROUPS, S5_GROUP)
    uc = u_ctx.astype(F32).reshape(bsz, u_ctx.shape[1], S5_GROUPS, S5_GROUP)
    ys_l, ys_c = [], []
    for k, order in ((0, _ident), (1, _rev)):
        disc = s5_discretise(lam_re[k], lam_im[k], log_step[k], b_re, b_im)
        sc_re, sc_im = s5_states(order(uc), disc)
        sl_re, sl_im = s5_states(order(ul), disc, (sc_re[:, -1], sc_im[:, -1]))
        ys_l.append(order(s5_readout(sl_re, sl_im, c_re[k], c_im[k])))
        if ctx_out:
            ys_c.append(order(s5_readout(sc_re, sc_im, c_re[k], c_im[k])))
    y_lat = s5_glu(ys_l[0] + ys_l[1], ul, d, w_glu).astype(u_lat.dtype)
    y_ctx = s5_glu(ys_c[0] + ys_c[1], uc, d, w_glu).astype(u_ctx.dtype) if ctx_out else None
    return y_lat, y_ctx


def segsum_exp(cum):
    t = cum.shape[-1]
    mask = jnp.tril(jnp.ones((t, t), dtype=bool))
    diff = cum[..., :, None] - cum[..., None, :]
    return jnp.where(mask, jnp.exp(jnp.where(mask, diff, 0.0)), 0.0)


def ssd_chunked(xh, dt, a, bm, cm, s0):
    bsz, n, _, hp = xh.shape
    grp, ns = bm.shape[2], bm.shape[3]
    rep = SSD_HEADS // grp
    nc = n // SSD_CHUNK
    xc = (xh * dt[..., None]).reshape(bsz, nc, SSD_CHUNK, grp, rep, hp)
    la = jnp.moveaxis((dt * a).reshape(bsz, nc, SSD_CHUNK, grp, rep), 2, -1)
    cum = jnp.cumsum(la, axis=-1)
    bc = bm.reshape(bsz, nc, SSD_CHUNK, grp, ns)
    cc = cm.reshape(bsz, nc, SSD_CHUNK, grp, ns)
    cb = jnp.einsum('bclgn,bcsgn->bcgls', cc, bc)
    scores = cb[:, :, :, None] * segsum_exp(cum)
    y = jnp.einsum('bcgrls,bcsgrp->bclgrp', scores, xc)
    decay_to_end = jnp.exp(cum[..., -1:] - cum)
    states = jnp.einsum('bclgn,bcgrl,bclgrp->bcgrpn', bc, decay_to_end, xc)
    states = jnp.concatenate([s0[:, None], states], axis=1)
    chunk_cum = jnp.cumsum(jnp.pad(cum[..., -1], ((0, 0), (1, 0), (0, 0), (0, 0))), axis=1)
    decay_chunk = segsum_exp(jnp.moveaxis(chunk_cum, 1, -1))
    states_in = jnp.einsum('bgrzc,bcgrpn->bzgrpn', decay_chunk[..., :-1, :], states)
    y = y + jnp.einsum('bclgn,bcgrpn,bcgrl->bclgrp', cc, states_in, jnp.exp(cum))
    return y.reshape(bsz, n, SSD_HEADS, hp)


def ssd_final_state(xh, dt, a, bm):
    bsz, n, _, hp = xh.shape
    grp = bm.shape[2]
    rep = SSD_HEADS // grp
    cum = jnp.cumsum(dt * a, axis=1)
    decay = jnp.exp(cum[:, -1:] - cum).reshape(bsz, n, grp, rep)
    xd = (xh * dt[..., None]).reshape(bsz, n, grp, rep, hp)
    return jnp.einsum('blgn,blgr,blgrp->bgrpn', bm, decay, xd)


def ssd_prepare(xbc, dtr, conv_w, conv_b, dt_bias):
    xbc = jax.nn.silu(dwconv_seq(xbc, conv_w, conv_b)).astype(F32)
    bsz, n, _ = xbc.shape
    gn = SSD_GROUPS * SSD_STATE
    xh = xbc[..., :SSD_WIDTH].reshape(bsz, n, SSD_HEADS, SSD_HEAD_DIM)
    bm = xbc[..., SSD_WIDTH:SSD_WIDTH + gn].reshape(bsz, n, SSD_GROUPS, SSD_STATE)
    cm = xbc[..., SSD_WIDTH + gn:].reshape(bsz, n, SSD_GROUPS, SSD_STATE)
    dt = jax.nn.softplus(dtr.astype(F32).reshape(bsz, n, 2, SSD_HEADS) + dt_bias.astype(F32))
    return xh, bm, cm, dt


def ssd_output(y, xh, z, d, norm_w):
    y = (y + xh * d.astype(F32)[:, None]).reshape(z.shape)
    h = y * jax.nn.silu(z.astype(F32))
    hg = h.reshape(h.shape[0], h.shape[1], SSD_GROUPS, SSD_WIDTH // SSD_GROUPS)
    hg = hg * lax.rsqrt(jnp.mean(jnp.square(hg), axis=-1, keepdims=True) + RMS_EPS)
    return (hg.reshape(h.shape) * norm_w.astype(F32)).astype(z.dtype)


def ssd_branch(xbc_l, dtr_l, z_l, xbc_c, dtr_c, z_c, conv_w, conv_b, dt_bias, a_log, d, norm_w):
    a = -jnp.exp(a_log.astype(F32))
    xl, bl, cl, dtl = ssd_prepare(xbc_l, dtr_l, conv_w, conv_b, dt_bias)
    xc, bc, cc, dtc = ssd_prepare(xbc_c, dtr_c, conv_w, conv_b, dt_bias)
    ys_l, ys_c = [], []
    for k, order in ((0, _ident), (1, _rev)):
        s_ctx = ssd_final_state(order(xc), order(dtc[:, :, k]), a[k], order(bc))
        ys_l.append(order(ssd_chunked(order(xl), order(dtl[:, :, k]), a[k], order(bl), order(cl), s_ctx)))
        if z_c is not None:
            ys_c.append(order(ssd_chunked(order(xc), order(dtc[:, :, k]), a[k], order(bc), order(cc),
                                          jnp.zeros_like(s_ctx))))
    y_lat = ssd_output(ys_l[0] + ys_l[1], xl, z_l, d, norm_w)
    y_ctx = ssd_output(ys_c[0] + ys_c[1], xc, z_c, d, norm_w) if z_c is not None else None
    return y_lat, y_ctx


def merge_branches(y_s5, y_ssd, gates, w_s5p, w_ssdp, w_out):
    g = jax.nn.sigmoid(gates.astype(F32)).astype(gates.dtype)
    merged = g[..., :D_MODEL] * (y_s5 @ w_s5p) + g[..., D_MODEL:] * (y_ssd @ w_ssdp)
    return merged @ w_out


def conv_ffn(h, w_up, conv_w, conv_b, w_down, grid):
    up = h @ w_up
    gate, val = up[..., :D_FF], up[..., D_FF:]
    gate = dwconv_grid(gate, conv_w, conv_b) if grid else dwconv_seq(gate, conv_w[FFN_CONV // 2], conv_b)
    return (jax.nn.silu(gate) * val) @ w_down


def setup_inputs(seed: int = 0) -> dict:
    key = jax.random.key(seed)
    kit = iter(jax.random.split(key, 48))

    def normal(shape, scale):
        return scale * jax.random.normal(next(kit), shape, F32)

    L = DEPTH
    x = normal((BATCH, SEQ, D_MODEL), 1.0)
    c = normal((BATCH, D_MODEL), 1.0)
    ctx = normal((BATCH, CTX_LEN, D_MODEL), 1.0)
    c_ctx = normal((D_MODEL,), 1.0)
    w_ada = normal((L, D_MODEL, 6 * D_MODEL), 0.5 * D_MODEL ** -0.5)
    b_ada = normal((L, 6 * D_MODEL), 0.01)
    w_in = normal((L, D_MODEL, PROJ_COLS), D_MODEL ** -0.5)
    n_idx = jnp.arange(S5_STATE, dtype=F32)
    s5_lam_re = -0.5 + normal((L, 2, S5_GROUPS, S5_STATE), 0.01)
    s5_lam_im = jnp.pi * n_idx + normal((L, 2, S5_GROUPS, S5_STATE), 0.01)
    s5_log_step = jax.random.uniform(next(kit), (L, 2, S5_GROUPS), F32, math.log(1e-3), math.log(1e-1))
    s5_b_re = normal((L, S5_GROUPS, S5_STATE, S5_GROUP), (2 * S5_GROUP) ** -0.5)
    s5_b_im = normal((L, S5_GROUPS, S5_STATE, S5_GROUP), (2 * S5_GROUP) ** -0.5)
    s5_c_re = normal((L, 2, S5_GROUPS, S5_GROUP, S5_STATE), 0.5)
    s5_c_im = normal((L, 2, S5_GROUPS, S5_GROUP, S5_STATE), 0.5)
    s5_d = normal((L, S5_WIDTH), 1.0)
    s5_w_glu = normal((L, S5_WIDTH, 2 * S5_WIDTH), S5_WIDTH ** -0.5)
    s5_w_proj = normal((L, S5_WIDTH, D_MODEL), DEEPNORM_BETA * S5_WIDTH ** -0.5)
    ssd_conv_w = normal((L, SSD_CONV, SSD_CONV_CH), SSD_CONV ** -0.5)
    ssd_conv_b = normal((L, SSD_CONV_CH), 0.01)
    dt0 = jnp.exp(jax.random.uniform(next(kit), (L, 2, SSD_HEADS), F32, math.log(1e-3), math.log(1e-1)))
    ssd_dt_bias = dt0 + jnp.log(-jnp.expm1(-dt0))
    ssd_a_log = jnp.log(jax.random.uniform(next(kit), (L, 2, SSD_HEADS), F32, 1.0, 16.0))
    ssd_d = 1.0 + normal((L, SSD_HEADS), 0.1)
    ssd_norm_w = 1.0 + normal((L, SSD_WIDTH), 0.01)
    ssd_w_proj = normal((L, SSD_WIDTH, D_MODEL), DEEPNORM_BETA * SSD_WIDTH ** -0.5)
    w_out = normal((L, D_MODEL, D_MODEL), DEEPNORM_BETA * D_MODEL ** -0.5)
    ln1_g = 1.0 + normal((L, D_MODEL), 0.01)
    ln1_b = normal((L, D_MODEL), 0.01)
    w_up = normal((L, D_MODEL, 2 * D_FF), D_MODEL ** -0.5)
    ffn_conv_w = normal((L, FFN_CONV, FFN_CONV, D_FF), 1.0 / FFN_CONV)
    ffn_conv_b = normal((L, D_FF), 0.01)
    w_down = normal((L, D_FF, D_MODEL), DEEPNORM_BETA * D_FF ** -0.5)
    ln2_g = 1.0 + normal((L, D_MODEL), 0.01)
    ln2_b = normal((L, D_MODEL), 0.01)
    return {'x': x, 'c': c, 'ctx': ctx, 'c_ctx': c_ctx, 'w_ada': w_ada, 'b_ada': b_ada, 'w_in': w_in,
            's5_lam_re': s5_lam_re, 's5_lam_im': s5_lam_im, 's5_log_step': s5_log_step,
            's5_b_re': s5_b_re, 's5_b_im': s5_b_im, 's5_c_re': s5_c_re, 's5_c_im': s5_c_im,
            's5_d': s5_d, 's5_w_glu': s5_w_glu, 's5_w_proj': s5_w_proj,
            'ssd_conv_w': ssd_conv_w, 'ssd_conv_b': ssd_conv_b, 'ssd_dt_bias': ssd_dt_bias,
            'ssd_a_log': ssd_a_log, 'ssd_d': ssd_d, 'ssd_norm_w': ssd_norm_w, 'ssd_w_proj': ssd_w_proj,
            'w_out': w_out, 'ln1_g': ln1_g, 'ln1_b': ln1_b, 'w_up': w_up, 'ffn_conv_w': ffn_conv_w,
            'ffn_conv_b': ffn_conv_b, 'w_down': w_down, 'ln2_g': ln2_g, 'ln2_b': ln2_b}


def reference(x, c, ctx, c_ctx, w_ada, b_ada, w_in, s5_lam_re, s5_lam_im, s5_log_step, s5_b_re, s5_b_im,
              s5_c_re, s5_c_im, s5_d, s5_w_glu, s5_w_proj, ssd_conv_w, ssd_conv_b, ssd_dt_bias, ssd_a_log,
              ssd_d, ssd_norm_w, ssd_w_proj, w_out, ln1_g, ln1_b, w_up, ffn_conv_w, ffn_conv_b, w_down,
              ln2_g, ln2_b):
    h_lat, h_ctx = x, ctx
    for i in range(DEPTH):
        last = i == DEPTH - 1
        n_mod_ctx = (2 if last else 6) * D_MODEL
        m_lat = adaln(c, w_ada[i], b_ada[i])[:, None, :]
        sh1, sc1, g1, sh2, sc2, g2 = jnp.split(m_lat, 6, axis=-1)
        m_ctx = adaln(c_ctx, w_ada[i][:, :n_mod_ctx], b_ada[i][:n_mod_ctx])

        p_lat = modulate(h_lat, sh1, sc1) @ w_in[i]
        w_in_ctx = w_in[i][:, :COL_Z] if last else w_in[i]
        p_ctx = modulate(h_ctx, m_ctx[:D_MODEL], m_ctx[D_MODEL:2 * D_MODEL]) @ w_in_ctx
        y5_l, y5_c = s5_branch(p_lat[..., :S5_WIDTH], p_ctx[..., :S5_WIDTH], s5_lam_re[i], s5_lam_im[i],
                               s5_log_step[i], s5_b_re[i], s5_b_im[i], s5_c_re[i], s5_c_im[i], s5_d[i],
                               s5_w_glu[i], not last)
        z_ctx = None if last else p_ctx[..., COL_Z:COL_GATES]
        yd_l, yd_c = ssd_branch(p_lat[..., COL_XBC:COL_DT], p_lat[..., COL_DT:COL_Z], p_lat[..., COL_Z:COL_GATES],
                                p_ctx[..., COL_XBC:COL_DT], p_ctx[..., COL_DT:COL_Z], z_ctx,
                                ssd_conv_w[i], ssd_conv_b[i], ssd_dt_bias[i], ssd_a_log[i], ssd_d[i],
                                ssd_norm_w[i])
        out_l = merge_branches(y5_l, yd_l, p_lat[..., COL_GATES:], s5_w_proj[i], ssd_w_proj[i], w_out[i])
        h_lat = layer_norm(DEEPNORM_ALPHA * h_lat + g1 * out_l, ln1_g[i], ln1_b[i])

        f_l = conv_ffn(modulate(h_lat, sh2, sc2), w_up[i], ffn_conv_w[i], ffn_conv_b[i], w_down[i], True)
        h_lat = layer_norm(DEEPNORM_ALPHA * h_lat + g2 * f_l, ln2_g[i], ln2_b[i])

        if not last:
            c_g1, c_sh2, c_sc2, c_g2 = (m_ctx[2 * D_MODEL:3 * D_MODEL], m_ctx[3 * D_MODEL:4 * D_MODEL],
                                        m_ctx[4 * D_MODEL:5 * D_MODEL], m_ctx[5 * D_MODEL:])
            out_c = merge_branches(y5_c, yd_c, p_ctx[..., COL_GATES:], s5_w_proj[i], ssd_w_proj[i], w_out[i])
            h_ctx = layer_norm(DEEPNORM_ALPHA * h_ctx + c_g1 * out_c, ln1_g[i], ln1_b[i])
            f_c = conv_ffn(modulate(h_ctx, c_sh2, c_sc2), w_up[i], ffn_conv_w[i], ffn_conv_b[i], w_down[i], False)
            h_ctx = layer_norm(DEEPNORM_ALPHA * h_ctx + c_g2 * f_c, ln2_g[i], ln2_b[i])
    return h_lat
```

```python
import math
from contextlib import ExitStack
import numpy as np
import concourse.bass as bass
import concourse.mybir as mybir
from concourse.bass_utils import run_bass_kernel_spmd

F32 = mybir.dt.float32
BF16 = mybir.dt.bfloat16
I32 = mybir.dt.int32
AF = mybir.ActivationFunctionType
ALU = mybir.AluOpType

NCORES = 8
D = 2048
KD = 16
SEQ = 16384
TL = SEQ // NCORES
TC = 256
NT = TC + TL
DEPTH = 2
S5W = 1024
SSDW = 2048
NHEAD = 32
HDIM = 64
NSTATE = 128
DFF = 5632
FT = DFF // 128
COL_XBC = 1024
COL_DT = COL_XBC + 3072
COL_Z = COL_DT + 64
COL_G = COL_Z + 2048
PROJ = COL_G + 4096
ALPHA = (2 * DEPTH) ** 0.25
LN_EPS = 1e-5
RMS_EPS = 1e-5
GRID_W = 64


class Buf:
    __slots__ = ("w", "r", "name")

    def __init__(self, name=""):
        self.w = None
        self.r = {}
        self.name = name


class Sched:
    ENG = ("pe", "act", "dve", "pool", "sp")

    def __init__(self, nc, es):
        self.nc = nc
        self.es = es
        self.h = {"pe": nc.tensor, "act": nc.scalar, "dve": nc.vector, "pool": nc.gpsimd, "sp": nc.sync}
        self.sem = {}
        self.cnt = {}
        self.waited = {e: {} for e in self.ENG}
        self.pending = {e: [] for e in self.ENG}
        self.ncc = 0
        for e in self.ENG:
            self.newsem(e)

    def newsem(self, name):
        self.sem[name] = self.es.enter_context(self.nc.semaphore("sm_" + name))
        self.cnt[name] = 0

    def _wait(self, eng, deps):
        for sm, v in deps.items():
            if eng == "pe" and sm == "pe":
                continue
            if self.waited[eng].get(sm, 0) < v:
                self.h[eng].wait_ge(self.sem[sm], v)
                self.waited[eng][sm] = v

    @staticmethod
    def _deps(reads, writes):
        d = {}

        def add(sm, v):
            if d.get(sm, 0) < v:
                d[sm] = v

        for b in reads:
            if b.w is not None:
                add(*b.w)
        for b in writes:
            if b.w is not None:
                add(*b.w)
            for sm, v in b.r.items():
                add(sm, v)
        return d

    @staticmethod
    def _mark(ev, reads, writes):
        sm, v = ev
        for b in reads:
            if b.r.get(sm, 0) < v:
                b.r[sm] = v
        for b in writes:
            b.w = ev
            b.r = {}

    def op(self, eng, fn, reads=(), writes=(), inc=True):
        self._wait(eng, self._deps(reads, writes))
        inst = fn(self.h[eng])
        if not inc:
            self.pending[eng].append((tuple(reads), tuple(writes)))
            return
        self.cnt[eng] += 1
        inst.then_inc(self.sem[eng], 1)
        ev = (eng, self.cnt[eng])
        for r, w in self.pending[eng]:
            self._mark(ev, r, w)
        self.pending[eng] = []
        self._mark(ev, reads, writes)

    NDMASEM = 40

    def dma(self, q, out, in_, reads=(), writes=(), stream="ld", slow=False):
        idx = getattr(self, "_dmai", 0)
        self._dmai = idx + 1
        name = "dq%d" % (idx % self.NDMASEM)
        if name not in self.sem:
            self.newsem(name)
        deps = self._deps(reads, writes)
        if self.cnt[name] > 0:
            deps[name] = max(deps.get(name, 0), self.cnt[name])
        self._wait(q, deps)
        if slow:
            inst = self.h[q].dma_start(out=out, in_=in_, allow_slow_non_contiguous=True)
        else:
            inst = self.h[q].dma_start(out=out, in_=in_)
        self.cnt[name] += 16
        inst.then_inc(self.sem[name], 16)
        self._mark((name, self.cnt[name]), reads, writes)

    def allgather(self, src_t, dst_t, reads, writes, dummy):
        self._wait("pool", self._deps(reads, writes))
        name = "cc%d" % self.ncc
        self.ncc += 1
        sem = self.es.enter_context(self.nc.semaphore("sm_" + name))
        self.h["pool"].collective_compute(
            "AllGather", ALU.bypass, replica_groups=[list(range(NCORES))],
            ins=[src_t.ap().opt()], outs=[dst_t.ap().opt()]).then_inc(sem)
        self.h["pool"].wait_ge(sem, 1)
        self.op("pool", lambda e: e.memset(dummy, 0.0), reads=reads, writes=writes)

    def barrier(self):
        allc = {k: v for k, v in self.cnt.items() if v > 0}
        for e in self.ENG:
            self._wait(e, dict(allc))


def TT(out, a, b, op):
    return lambda e: e.tensor_tensor(out=out, in0=a, in1=b, op=op)


def TS(out, a, s1, s2, op0, op1=None):
    if op1 is None:
        return lambda e: e.tensor_scalar(out=out, in0=a, scalar1=s1, scalar2=None, op0=op0)
    return lambda e: e.tensor_scalar(out=out, in0=a, scalar1=s1, scalar2=s2, op0=op0, op1=op1)


def STT(out, a, s, b, op0, op1):
    return lambda e: e.scalar_tensor_tensor(out=out, in0=a, scalar=s, in1=b, op0=op0, op1=op1)


def ACT(out, in_, func, bias=None, scale=1.0):
    if bias is None:
        return lambda e: e.activation(out=out, in_=in_, func=func, scale=scale)
    return lambda e: e.activation(out=out, in_=in_, func=func, bias=bias, scale=scale)


def CP(out, in_):
    return lambda e: e.tensor_copy(out, in_)


def MM(out, lhsT, rhs, start, stop):
    return lambda e: e.matmul(out, lhsT, rhs, start=start, stop=stop)


def TR(out, in_, ident):
    return lambda e: e.transpose(out, in_, ident)


class V:
    def __init__(self, t, name):
        self.t = t
        self.b = Buf(name)

    def __getitem__(self, k):
        return self.t[k]


T = V


def _prod(xs):
    r = 1
    for x in xs:
        r *= int(x)
    return r


class Arena:
    def __init__(self, ap, nwords):
        self.ap = ap
        self.lo = 0
        self.hi = nwords

    def alloc(self, name, shape, dt=F32, hi=False):
        n = _prod(shape[1:])
        words = n if dt in (F32, I32) else (n + 1) // 2
        words = (words + 1) // 2 * 2
        if hi:
            self.hi -= words
            off = self.hi
        else:
            off = self.lo
            self.lo += words
        assert self.lo <= self.hi, "SBUF arena overflow at %s (lo=%d hi=%d)" % (name, self.lo, self.hi)
        ap = self.ap[0:shape[0], off:off + words]
        if dt != F32:
            ap = ap.bitcast(dt)
        ap = ap[:, 0:n]
        if len(shape) == 3:
            ap = ap.rearrange("p (a b) -> p a b", b=shape[2])
        elif len(shape) == 4:
            ap = ap.rearrange("p (a b c) -> p a b c", b=shape[2], c=shape[3])
        elif len(shape) == 5:
            ap = ap.rearrange("p (a b c d) -> p a b c d", b=shape[2], c=shape[3], d=shape[4])
        return V(ap, name)


class Scope:
    def __init__(self, prog):
        self.p = prog

    def __enter__(self):
        self.m = (self.p.A.lo, self.p.A.hi)
        return self

    def __exit__(self, *a):
        self.p.S.barrier()
        self.p.A.lo, self.p.A.hi = self.m
        return False


class Prog:
    def __init__(self, nc, dbg=None, nlayers=DEPTH):
        self.nc = nc
        self.dbg = dbg or ()
        self.nlayers = nlayers
        self.es = ExitStack()
        self.S = Sched(nc, self.es)
        self.dram = {}
        self.psi = 0
        self.gath = {}

    def sb(self, es, name, shape, dt=F32, hi=False):
        return self.A.alloc(name, shape, dt, hi=hi)

    def scope(self):
        return Scope(self)

    def din(self, name, shape, dt=F32):
        t = self.nc.dram_tensor(name, list(shape), dt, kind="ExternalInput")
        self.dram[name] = V(t, name)
        return self.dram[name]

    def dscr(self, name, shape, dt=F32):
        if name in self.dbg:
            t = self.nc.dram_tensor(name, list(shape), dt, kind="ExternalOutput")
        else:
            t = self.nc.dram_tensor(name, list(shape), dt)
        self.dram[name] = V(t, name)
        return self.dram[name]

    def psum(self):
        p = self.ps[self.psi % 8]
        self.psi += 1
        return p

    def wdecl(self, name, rows, cols):
        self.din(name + "_sh", [rows // NCORES, cols])
        self.dscr(name + "_loc", [rows // NCORES, cols])
        self.dscr(name, [rows, cols])

    def gather_issue(self, name):
        S, dr = self.S, self.dram
        sh, loc, full = dr[name + "_sh"], dr[name + "_loc"], dr[name]
        S.dma("pool", loc[:, :], sh[:, :], writes=[loc.b], stream="g2g")
        S._wait("pool", S._deps([loc.b], [full.b]))
        sem = self.es.enter_context(self.nc.semaphore("sm_g_" + name))
        S.h["pool"].collective_compute("AllGather", ALU.bypass, replica_groups=[list(range(NCORES))],
                                       ins=[loc.t.ap().opt()], outs=[full.t.ap().opt()]).then_inc(sem)
        self.gath[name] = sem

    def gather_wait(self, name):
        if name not in self.gath:
            return
        S = self.S
        sem = self.gath.pop(name)
        S.h["pool"].wait_ge(sem, 1)
        S.op("pool", lambda e: e.memset(self.dummy[:], 0.0), writes=[self.dram[name].b, self.dummy.b])

    def declare(self):
        d = self.din
        d("x_fm", [D, TL]); d("ctx_fm", [D, TC]); d("xhalo", [D, 2]); d("cc", [128, 32])
        d("rank", [128, 32])
        d("ident", [128, 128]); d("masks", [128, 2, 128]); d("esel", [8, 2, 128]); d("tri", [128, 2, 128])
        d("w_ada_sl", [DEPTH, D, 1536]); d("b_ada_sl", [128, DEPTH, 12])
        d("ssd_conv_w", [128, DEPTH, 24, 3]); d("ssd_conv_b", [128, DEPTH, 24])
        d("dtb_alog", [128, DEPTH, 4]); d("ssd_dvec", [128, DEPTH, 16]); d("ssd_nw", [128, DEPTH, 16])
        d("s5_lamc", [128, DEPTH, 3, 64]); d("s5_cc", [128, DEPTH, 2, 32, 2, 16]); d("s5_bt", [16, DEPTH, 2, 4096])
        d("s5_dcol", [128, DEPTH, 64])
        d("ln_gb", [128, DEPTH, 4, 16]); d("ffn_cw", [128, DEPTH, FT, 9]); d("ffn_cb", [128, DEPTH, FT])
        for l in range(DEPTH):
            self.wdecl("w_in%d" % l, D, PROJ)
            self.wdecl("w_glu%d" % l, S5W, 2 * S5W)
            self.wdecl("w_s5p%d" % l, S5W, D)
            self.wdecl("w_ssdp%d" % l, SSDW, D)
            self.wdecl("w_out%d" % l, D, D)
            self.wdecl("w_up%d" % l, D, 2 * DFF)
            self.wdecl("w_down%d" % l, DFF, D)
        self.out = V(self.nc.dram_tensor("y_fm", [D, TL], F32, kind="ExternalOutput"), "y_fm")
        s = self.dscr
        s("modloc", [128, 48]); s("modfull", [NCORES * 128, 48])
        s("u_d", [S5W, NT], BF16)
        s("xT_d", [SSDW, NT], BF16); s("bcT_d", [1024, NT], BF16)
        s("dt_d", [128, NT]); s("zs_d", [SSDW, NT]); s("g_d", [2 * D, NT])
        s("hlat_d", [D, TL]); s("hctx_d", [D, TC])
        s("hhloc", [D, 2]); s("hhfull", [NCORES * D, 2]); s("hhalo_d", [D, 2])
        s("s5loc", [128, 128]); s("s5full", [NCORES * 128, 128])
        s("cum_d", [64, NT]); s("dec_d", [64, NT]); s("dcloc", [64, 1]); s("dcfull", [NCORES * 64, 1])
        s("sloc_d", [NT // 128, 128, 4096]); s("sin_d", [NT // 128, 128, 4096], BF16)
        s("ssdloc", [128, 4096]); s("ssdfull", [NCORES * 128, 4096])
        s("yssd_d", [SSDW, NT]); s("pre_d", [D, NT]); s("h1_d", [D, NT]); s("pre2_d", [D, NT])
        s("xm2_d", [D, 64 + TL + 64 + TC], BF16)
        s("hxloc", [D, 128], BF16); s("hxfull", [NCORES * D, 128], BF16)
        s("glu_d", [S5W, NT], BF16); s("hn_d", [SSDW, NT], BF16)

    def build(self, stop=None):
        nc, S = self.nc, self.S
        self.declare()
        es = self.es
        self.ps = [V(es.enter_context(nc.psum_tensor("psb%d" % i, [128, 512], F32)), "ps%d" % i) for i in range(8)]
        NW = 52800
        araw = es.enter_context(nc.sbuf_tensor("arena", [128, NW], F32))
        self.A = Arena(araw[:], NW)
        self.dummy = self.sb(None, "dummy", [128, 8], hi=True)
        self.rank = self.sb(None, "rank", [128, 32], hi=True)
        self.ident = self.sb(None, "ident", [128, 128], hi=True)
        self.identb = self.sb(None, "identb", [128, 128], BF16, hi=True)
        self.mod = self.sb(None, "mod", [128, DEPTH, 96, 2], hi=True)
        self.ops = self.sb(None, "ops", [128, DEPTH, 2, 16, 2], hi=True)
        self.onesb = self.sb(None, "onesb", [128, 128], BF16, hi=True)
        self.lngb = self.sb(None, "lngb", [128, DEPTH, 4, 16], hi=True)
        self.nwt = self.sb(None, "nwt", [128, 16], hi=True)
        S.op("dve", lambda e: e.memset(self.onesb[:], 1.0), writes=[self.onesb.b])
        S.dma("sp", self.lngb[:], self.dram["ln_gb"][:, :, :, :], writes=[self.lngb.b])
        S.dma("sp", self.rank[:], self.dram["rank"][:, :], writes=[self.rank.b])
        S.dma("sp", self.ident[:], self.dram["ident"][:, :], writes=[self.ident.b])
        S.op("dve", CP(self.identb[:], self.ident[:]), reads=[self.ident.b], writes=[self.identb.b])
        self.stop = stop
        self.gather_issue("w_in0")
        self.phase0()
        for l in range(self.nlayers):
            if self.layer(l):
                break
        S.barrier()
        es.close()

    def layer(self, l):
        stop = self.stop
        self.l = l
        S = self.S
        for nm in ("w_glu", "w_s5p", "w_ssdp", "w_out", "w_up", "w_down"):
            self.gather_issue("%s%d" % (nm, l))
        self.phaseA(l)
        if stop == (l, "A"):
            return True
        S.barrier()
        self.s5(l)
        if stop == (l, "S5"):
            return True
        self.ssd(l)
        if stop == (l, "SSD"):
            return True
        self.phaseC(l)
        S.barrier()
        self.A.lo, self.A.hi = self.mixmark
        if stop == (l, "C"):
            return True
        if l + 1 < self.nlayers:
            self.gather_issue("w_in%d" % (l + 1))
        self.phaseD(l)
        return False

    def phase0(self):
        nc, S, dr = self.nc, self.S, self.dram
        with self.scope() as es:
            cc = self.sb(es, "cc", [128, 32]); scs = self.sb(es, "scs", [128, 32])
            bt = self.sb(es, "bada", [128, DEPTH, 12])
            msl = self.sb(es, "msl", [128, DEPTH, 12, 2])
            wb = [self.sb(es, "wada%d" % i, [128, 16, 512]) for i in range(2)]
            S.dma("sp", cc[:], dr["cc"][:, :], writes=[cc.b])
            S.dma("sp", bt[:], dr["b_ada_sl"][:, :, :], writes=[bt.b])
            S.op("act", ACT(scs[:], cc[:], AF.Silu), reads=[cc.b], writes=[scs.b])
            ps = self.psum()
            n = 0
            for l in range(DEPTH):
                for cb in range(3):
                    w = wb[n % 2]
                    n += 1
                    S.dma("sp", w[:], dr["w_ada_sl"][l, :, cb * 512:(cb + 1) * 512].rearrange("(kc p) c -> p kc c", p=128),
                          writes=[w.b], stream="ldw")
                    for sub in range(4):
                        ct = l * 12 + cb * 4 + sub
                        for kc in range(16):
                            S.op("pe", MM(ps[:, ct * 2:ct * 2 + 2], w[:, kc, sub * 128:(sub + 1) * 128], scs[:, kc::16],
                                          kc == 0, kc == 15), reads=[w.b, scs.b], writes=[ps.b], inc=(kc == 15))
            S.op("dve", TT(msl[:].rearrange("p l c j -> p (l c) j"), ps[:, 0:48].rearrange("p (c j) -> p c j", j=2),
                           bt[:].rearrange("p l c -> p (l c)").unsqueeze(2).to_broadcast([128, 24, 2]), ALU.add),
                 reads=[ps.b, bt.b], writes=[msl.b])
            S.dma("sp", dr["modloc"][:, :], msl[:].rearrange("p l c j -> p (l c j)"), reads=[msl.b], writes=[dr["modloc"].b], stream="st")
            S.allgather(dr["modloc"].t, dr["modfull"].t, [dr["modloc"].b], [dr["modfull"].b], self.dummy[:])
            mod = self.mod
            for l in range(DEPTH):
                S.dma("sp", mod[:, l].rearrange("p (r c) j -> p r c j", c=12),
                      dr["modfull"][:, l * 24:(l + 1) * 24].rearrange("(r p) (c j) -> p r c j", p=128, j=2),
                      reads=[dr["modfull"].b], writes=[mod.b])
                S.op("dve", TS(self.ops[:, l, 0], mod[:, l, 16:32, :], 1.0, None, ALU.add), reads=[mod.b], writes=[self.ops.b])
                S.op("dve", TS(self.ops[:, l, 1], mod[:, l, 64:80, :], 1.0, None, ALU.add), reads=[mod.b], writes=[self.ops.b])

    def modv(self, v, kc, j):
        return self.mod[:, self.l, v * 16 + kc, j:j + 1]

    def phaseA(self, l):
        nc, S, dr = self.nc, self.S, self.dram
        last = (l == DEPTH - 1)
        NX = NT + 2
        self.mixmark = (self.A.lo, self.A.hi)
        self.u = self.sb(None, "u", [128, 8, NT], BF16, hi=True)
        with self.scope() as es:
            xm = self.sb(es, "xm", [128, 16, NX], BF16)
            xmb = [Buf("xm%d" % k) for k in range(16)]
            with self.scope() as es2:
                hb = [self.sb(es2, "hb%d" % i, [128, 16, 512]) for i in range(2)]
                if l == 0:
                    srcs = [(dr["ctx_fm"], 0, TC, 0, 1)] + [(dr["x_fm"], i * 512, 512, TC + i * 512, 0) for i in range(4)]
                    srcs.append((dr["xhalo"], 0, 2, NT, 0))
                else:
                    srcs = [(dr["hctx_d"], 0, TC, 0, 1)] + [(dr["hlat_d"], i * 512, 512, TC + i * 512, 0) for i in range(4)]
                    srcs.append((dr["hhalo_d"], 0, 2, NT, 0))
                for i, (src, t0, n, c0, j) in enumerate(srcs):
                    h = hb[i % 2]
                    S.dma("sp", h[:, :, :n], src[:, t0:t0 + n].rearrange("(kc p) t -> p kc t", p=128),
                          reads=[src.b], writes=[h.b], stream="lda")
                    for kc in range(16):
                        eng = "dve" if kc % 2 == 0 else "pool"
                        S.op(eng, TS(xm[:, kc, c0:c0 + n], h[:, kc, :n], self.ops[:, l, 0, kc, j:j + 1], self.modv(0, kc, j),
                                     ALU.mult, ALU.add), reads=[h.b, self.ops.b, self.mod.b], writes=[xmb[kc]])
            wbuf4 = [self.sb(es, "win4%d" % i, [128, 16, 512], BF16) for i in range(2)]
            wdt = self.sb(es, "windt", [128, 16, 128], BF16)
            stg = [self.sb(es, "stg%d" % i, [128, 512]) for i in range(3)]
            xrow = [self.sb(es, "xrow%d" % i, [128, NX]) for i in range(2)]
            acc = [self.sb(es, "cacc%d" % i, [128, NT]) for i in range(2)]
            sil = [self.sb(es, "sil%d" % i, [128, NT], BF16) for i in range(2)]
            cw = self.sb(es, "cw", [128, 24, 3]); cb = self.sb(es, "cb", [128, 24])
            S.dma("sp", cw[:], dr["ssd_conv_w"][:, l], writes=[cw.b])
            S.dma("sp", cb[:], dr["ssd_conv_b"][:, l], writes=[cb.b])
            blocks = [(0, TC)] + [(TC + i * 512, 512) for i in range(4)]
            tiles = [("u", i, i * 128) for i in range(8)] + [("dt", 0, COL_DT)]
            tiles += [("xbc", i, COL_XBC + i * 128) for i in range(24)]
            tiles += [("z", i, COL_Z + i * 128) for i in range(16)] + [("g", i, COL_G + i * 128) for i in range(32)]
            self.gather_wait("w_in%d" % l)
            win = dr["w_in%d" % l]
            nst = 0
            grp = {}
            ngrp = 0
            for ti, (kind, i, c0) in enumerate(tiles):
                if kind == "dt":
                    w = wdt
                    for hh in range(2):
                        S.dma("pool", w[:, :, hh * 64:(hh + 1) * 64],
                              win[:, c0:c0 + 64].rearrange("(kc p) c -> p kc c", p=128), reads=[win.b], writes=[w.b], stream="ldw")
                    wl = lambda kc, w=w: w[:, kc, :]
                else:
                    if i % 4 == 0:
                        wq = wbuf4[ngrp % 2]
                        ngrp += 1
                        S.dma("pool", wq[:], win[:, c0:c0 + 512].rearrange("(kc p) c -> p kc c", p=128), reads=[win.b], writes=[wq.b], stream="ldw")
                        grp["w"] = wq
                    w = grp["w"]
                    wl = lambda kc, w=w, j=i % 4: w[:, kc, j * 128:(j + 1) * 128]
                blks = list(blocks)
                if kind == "xbc":
                    blks.append((NT, 2))
                if last and kind in ("z", "g"):
                    blks = blks[1:]
                xr = xrow[i % 2]
                for (b0, n) in blks:
                    ps = self.psum()
                    for kc in range(16):
                        S.op("pe", MM(ps[:, :n], wl(kc), xm[:, kc, b0:b0 + n], kc == 0, kc == 15),
                             reads=[w.b, xmb[kc]], writes=[ps.b], inc=(kc == 15))
                    if kind == "u":
                        S.op("act", ACT(self.u[:, i, b0:b0 + n], ps[:, :n], AF.Copy), reads=[ps.b], writes=[self.u.b])
                    elif kind == "dt":
                        st = stg[nst % 3]
                        nst += 1
                        S.op("act", ACT(st[:, :n], ps[:, :n], AF.Copy), reads=[ps.b], writes=[st.b])
                        S.dma("sp", dr["dt_d"][:, b0:b0 + n], st[:, :n], reads=[st.b], writes=[dr["dt_d"].b], stream="st")
                    elif kind == "xbc":
                        S.op("act", ACT(xr[:, b0:b0 + n], ps[:, :n], AF.Copy), reads=[ps.b], writes=[xr.b])
                    else:
                        st = stg[nst % 3]
                        nst += 1
                        S.op("act", ACT(st[:, :n], ps[:, :n], AF.Silu if kind == "z" else AF.Sigmoid), reads=[ps.b], writes=[st.b])
                        dst = dr["zs_d"] if kind == "z" else dr["g_d"]
                        S.dma("sp", dst[i * 128:(i + 1) * 128, b0:b0 + n], st[:, :n], reads=[st.b], writes=[dst.b], stream="st")
                if kind == "xbc":
                    a = acc[i % 2]; sl = sil[i % 2]
                    w0, w1, w2, bb = cw[:, i, 0:1], cw[:, i, 1:2], cw[:, i, 2:3], cb[:, i:i + 1]
                    rd = [xr.b, cw.b, cb.b, a.b]
                    S.op("dve", TT(xr[:, NT:NT + 2], xr[:, NT:NT + 2], self.rank[:, 24:26], ALU.mult), reads=[xr.b, self.rank.b], writes=[xr.b])
                    S.op("dve", TS(a[:], xr[:, 0:NT], w1, bb, ALU.mult, ALU.add), reads=rd, writes=[a.b])
                    for (lo, hi) in ((0, TC), (TC, NT)):
                        S.op("dve", STT(a[:, lo + 1:hi], xr[:, lo:hi - 1], w0, a[:, lo + 1:hi], ALU.mult, ALU.add), reads=rd, writes=[a.b])
                        S.op("dve", STT(a[:, lo:hi - 1], xr[:, lo + 1:hi], w2, a[:, lo:hi - 1], ALU.mult, ALU.add), reads=rd, writes=[a.b])
                    S.op("dve", STT(a[:, TC:TC + 1], xr[:, NT:NT + 1], w0, a[:, TC:TC + 1], ALU.mult, ALU.add), reads=rd, writes=[a.b])
                    S.op("dve", STT(a[:, NT - 1:NT], xr[:, NT + 1:NT + 2], w2, a[:, NT - 1:NT], ALU.mult, ALU.add), reads=rd, writes=[a.b])
                    S.op("act", ACT(sl[:], a[:], AF.Silu), reads=[a.b], writes=[sl.b])
                    dst = dr["xT_d"] if i < 16 else dr["bcT_d"]
                    r0 = (i if i < 16 else i - 16) * 128
                    S.dma("sp", dst[r0:r0 + 128, :], sl[:], reads=[sl.b], writes=[dst.b], stream="st")
            if "u_d" in self.dbg:
                S.dma("sp", dr["u_d"][:, :].rearrange("(i p) t -> p i t", p=128), self.u[:], reads=[self.u.b], writes=[dr["u_d"].b], stream="st")

    def s5_tables(self, l):
        S, dr = self.S, self.dram
        cb = Buf("s5c")

        def al(name, shape=(128, 64), dt=F32):
            v = self.sb(None, name, list(shape), dt)
            v.b = cb
            return v

        def dv(fn):
            S.op("dve", fn, reads=[cb], writes=[cb])

        self.AK = AK = al("AK", (128, 16, 2, 64))
        self.AKq = AKq = al("AKq", (128, 8, 2, 64))
        self.cP = al("cP", (128, 2, 2, 32)); self.cQ = al("cQ", (128, 2, 2, 32))
        self.rP = al("rP", (128, 2, 2, 32)); self.rQ = al("rQ", (128, 2, 2, 32))
        self.s5cb = cb
        tmark = (self.A.lo, self.A.hi)
        lam = al("lam", (128, 3, 64))
        S.dma("sp", lam[:], dr["s5_lamc"][:, l], writes=[cb])
        lr, li, ls = lam[:, 0, :], lam[:, 1, :], lam[:, 2, :]
        step = al("step"); xr = al("xr"); th = al("th"); mag = al("mag")
        t1 = al("t1"); t2 = al("t2"); t3 = al("t3"); sn = al("sn"); cs = al("cs"); ni = al("ni", dt=I32)
        ar = al("ar"); ai = al("ai"); ivr = al("ivr"); ivi = al("ivi"); qr = al("qr"); qi = al("qi")
        dv(TS(lr, lr, -1e-4, None, ALU.min))
        S.op("act", ACT(step[:], ls, AF.Exp), reads=[cb], writes=[cb])
        dv(TT(xr[:], lr, step[:], ALU.mult)); dv(TT(th[:], li, step[:], ALU.mult))
        dv(TS(t1[:], xr[:], 0.125, None, ALU.mult))
        dv(TS(mag[:], t1[:], 1.0 / 720, 1.0 / 120, ALU.mult, ALU.add))
        for c in (1.0 / 24, 1.0 / 6, 0.5, 1.0, 1.0):
            dv(TT(mag[:], mag[:], t1[:], ALU.mult)); dv(TS(mag[:], mag[:], c, None, ALU.add))
        for _ in range(3):
            dv(TT(mag[:], mag[:], mag[:], ALU.mult))
        dv(TS(t1[:], th[:], 1.0 / (2 * math.pi), None, ALU.mult))
        dv(CP(ni[:], t1[:])); dv(CP(t2[:], ni[:]))
        C1 = 6.28125
        C2 = 2 * math.pi - C1
        dv(STT(t3[:], t2[:], -C1, th[:], ALU.mult, ALU.add))
        dv(STT(t3[:], t2[:], -C2, t3[:], ALU.mult, ALU.add))
        dv(TS(t3[:], t3[:], 0.125, None, ALU.mult))
        dv(TT(t1[:], t3[:], t3[:], ALU.mult))
        dv(TS(sn[:], t1[:], 1.0 / 362880, -1.0 / 5040, ALU.mult, ALU.add))
        for c in (1.0 / 120, -1.0 / 6, 1.0):
            dv(TT(sn[:], sn[:], t1[:], ALU.mult)); dv(TS(sn[:], sn[:], c, None, ALU.add))
        dv(TT(sn[:], sn[:], t3[:], ALU.mult))
        dv(TS(cs[:], t1[:], -1.0 / 3628800, 1.0 / 40320, ALU.mult, ALU.add))
        for c in (-1.0 / 720, 1.0 / 24, -0.5, 1.0):
            dv(TT(cs[:], cs[:], t1[:], ALU.mult)); dv(TS(cs[:], cs[:], c, None, ALU.add))
        for _ in range(3):
            dv(TT(t1[:], sn[:], cs[:], ALU.mult))
            dv(TT(t2[:], cs[:], cs[:], ALU.mult)); dv(TT(t3[:], sn[:], sn[:], ALU.mult))
            dv(TS(sn[:], t1[:], 2.0, None, ALU.mult)); dv(TT(cs[:], t2[:], t3[:], ALU.subtract))
        dv(TT(ar[:], mag[:], cs[:], ALU.mult)); dv(TT(ai[:], mag[:], sn[:], ALU.mult))
        dv(TT(t1[:], mag[:], mag[:], ALU.mult))
        S.op("dve", lambda e: e.reciprocal(t1[:], t1[:]), reads=[cb], writes=[cb])
        dv(TT(ivr[:], ar[:], t1[:], ALU.mult)); dv(STT(ivi[:], ai[:], -1.0, t1[:], ALU.mult, ALU.mult))
        dv(TT(t1[:], lr, lr, ALU.mult)); dv(TT(t2[:], li, li, ALU.mult)); dv(TT(t1[:], t1[:], t2[:], ALU.add))
        S.op("dve", lambda e: e.reciprocal(t1[:], t1[:]), reads=[cb], writes=[cb])
        dv(TS(t2[:], ar[:], -1.0, None, ALU.add))
        dv(TT(qr[:], t2[:], lr, ALU.mult)); dv(TT(t3[:], ai[:], li, ALU.mult)); dv(TT(qr[:], qr[:], t3[:], ALU.add))
        dv(TT(qr[:], qr[:], t1[:], ALU.mult))
        dv(TT(qi[:], ai[:], lr, ALU.mult)); dv(TT(t3[:], t2[:], li, ALU.mult)); dv(TT(qi[:], qi[:], t3[:], ALU.subtract))
        dv(TT(qi[:], qi[:], t1[:], ALU.mult))

        def cmul(o_r, o_i, a_r, a_i, b_r, b_i):
            dv(TT(t1[:], a_r, b_r, ALU.mult)); dv(TT(t2[:], a_i, b_i, ALU.mult)); dv(TT(o_r, t1[:], t2[:], ALU.subtract))
            dv(TT(t1[:], a_r, b_i, ALU.mult)); dv(TT(t2[:], a_i, b_r, ALU.mult)); dv(TT(o_i, t1[:], t2[:], ALU.add))

        S.op("dve", lambda e: e.memset(AK[:, 7, 0, :], 1.0), reads=[cb], writes=[cb])
        S.op("dve", lambda e: e.memset(AK[:, 7, 1, :], 0.0), reads=[cb], writes=[cb])
        for e_ in range(1, 9):
            cmul(AK[:, 7 + e_, 0, :], AK[:, 7 + e_, 1, :], AK[:, 6 + e_, 0, :], AK[:, 6 + e_, 1, :], ar[:], ai[:])
        for e_ in range(1, 8):
            cmul(AK[:, 7 - e_, 0, :], AK[:, 7 - e_, 1, :], AK[:, 8 - e_, 0, :], AK[:, 8 - e_, 1, :], ivr[:], ivi[:])
        for e_ in range(8):
            cmul(AKq[:, e_, 0, :], AKq[:, e_, 1, :], AK[:, 7 + e_, 0, :], AK[:, 7 + e_, 1, :], qr[:], qi[:])

        def coef(P, Q, a_r, a_i):
            v = lambda x: x.rearrange("p (d q) -> p d q", q=32)
            for ri in range(2):
                dv(CP(P[:, :, ri, :], v(a_r)))
            dv(TS(Q[:, :, 0, :], v(a_i), -1.0, None, ALU.mult)); dv(CP(Q[:, :, 1, :], v(a_i)))

        coef(self.cP, self.cQ, AK[:, 15, 0, :], AK[:, 15, 1, :])
        dv(CP(ar[:], AK[:, 15, 0, :])); dv(CP(ai[:], AK[:, 15, 1, :]))
        for _ in range(8):
            dv(TT(t1[:], ar[:], ar[:], ALU.mult)); dv(TT(t2[:], ai[:], ai[:], ALU.mult)); dv(TT(t3[:], ar[:], ai[:], ALU.mult))
            dv(TT(ar[:], t1[:], t2[:], ALU.subtract)); dv(TS(ai[:], t3[:], 2.0, None, ALU.mult))
        coef(self.rP, self.rQ, ar[:], ai[:])
        S.barrier()
        self.A.lo, self.A.hi = tmark

    def s5_rc(self, dst, cc, kf, kb, tmp1, tmp2, q_off=None):
        S = self.S
        AK = self.AK
        rd = [self.s5cb, cc.b, tmp1.b, tmp2.b]
        qlist = range(0, 32, 8) if q_off is None else range(q_off, q_off + 16, 8)
        dq = 0 if q_off is None else q_off
        for d, ks in ((0, kf), (1, kb)):
          for q0 in qlist:
            qs = slice(d * 32 + q0, d * 32 + q0 + 8)
            akr = AK[:, ks, 0, qs].rearrange("p t q -> p q t").unsqueeze(3).to_broadcast([128, 8, 8, 16])
            aki = AK[:, ks, 1, qs].rearrange("p t q -> p q t").unsqueeze(3).to_broadcast([128, 8, 8, 16])
            cre = cc[:, d, q0:q0 + 8, 0, :].unsqueeze(2).to_broadcast([128, 8, 8, 16])
            cim = cc[:, d, q0:q0 + 8, 1, :].unsqueeze(2).to_broadcast([128, 8, 8, 16])
            o_re = dst[:, d, q0 - dq:q0 - dq + 8, 0, :].rearrange("p q (t j) -> p q t j", j=16)
            o_im = dst[:, d, q0 - dq:q0 - dq + 8, 1, :].rearrange("p q (t j) -> p q t j", j=16)
            v1 = tmp1[:, 0:1024].rearrange("p (q t j) -> p q t j", t=8, j=16)
            v2 = tmp2[:, 0:1024].rearrange("p (q t j) -> p q t j", t=8, j=16)
            S.op("dve", TT(v1, akr, cre, ALU.mult), reads=rd, writes=[tmp1.b])
            S.op("pool", TT(v2, aki, cim, ALU.mult), reads=rd, writes=[tmp2.b])
            S.op("dve", TT(o_re, v1, v2, ALU.subtract), reads=rd, writes=[dst.b])
            S.op("dve", TT(v1, akr, cim, ALU.mult), reads=rd + [dst.b], writes=[tmp1.b])
            S.op("pool", TT(v2, aki, cre, ALU.mult), reads=rd + [dst.b], writes=[tmp2.b])
            S.op("dve", STT(o_im, v1, -1.0, v2, ALU.mult, ALU.subtract), reads=rd, writes=[dst.b])

    def s5(self, l):
        nc, S, dr = self.nc, self.S, self.dram
        last = (l == DEPTH - 1)
        u = self.u
        NB = NT // 8
        ub = [Buf("u%d" % i) for i in range(8)]
        for b in ub:
            b.w = u.b.w
        U = lambda g: u[:, g // 8, (g % 8) * NB:(g % 8 + 1) * NB]
        with self.scope() as es:
            self.s5_tables(l)
            M = self.sb(es, "s5M", [128, 64, 128], BF16)
            himark = self.A.hi
            Ws = self.sb(es, "Ws", [128, 2, 2, 4096], BF16, hi=True)
            cc = self.sb(es, "s5cc", [128, 2, 32, 2, 16])
            PselT = self.sb(es, "PselT", [128, 64, 128], BF16)
            S.dma("sp", cc[:], dr["s5_cc"][:, l], writes=[cc.b])
            with self.scope() as es2:
                Bm = self.sb(es2, "Bm", [128, 2, 4096])
                for s_ in range(8):
                    S.dma("sp", Bm[s_ * 16:(s_ + 1) * 16], dr["s5_bt"][:, l], writes=[Bm.b])
                esel = self.sb(es2, "esel", [8, 2, 128])
                S.dma("sp", esel[:], dr["esel"][:, :, :], writes=[esel.b])
                akt = [self.sb(es2, "akt%d" % r, [8, 4096]) for r in range(2)]
                tA = self.sb(es2, "tA", [128, 512]); tB = self.sb(es2, "tB", [128, 512])
                for d in range(2):
                    for ri in range(2):
                        for c in range(8):
                            ps = self.psum()
                            for k in range(4):
                                q = d * 32 + c * 4 + k
                                S.op("pe", TR(ps[0:8, k * 128:(k + 1) * 128], self.AKq[:, 0:8, ri, q], self.ident[:]),
                                     reads=[self.s5cb, self.ident.b], writes=[ps.b])
                            S.op("act", ACT(akt[ri][:, c * 512:(c + 1) * 512], ps[0:8, :], AF.Copy), reads=[ps.b], writes=[akt[ri].b])
                    for c in range(8):
                        pr, pi = self.psum(), self.psum()
                        S.op("pe", MM(pr[:, :], esel[:, d, :], akt[0][:, c * 512:(c + 1) * 512], True, True), reads=[esel.b, akt[0].b], writes=[pr.b])
                        S.op("pe", MM(pi[:, :], esel[:, d, :], akt[1][:, c * 512:(c + 1) * 512], True, True), reads=[esel.b, akt[1].b], writes=[pi.b])
                        cs_ = slice(c * 512, (c + 1) * 512)
                        S.op("dve", TT(tA[:], pr[:, :], Bm[:, 0, cs_], ALU.mult), reads=[pr.b, Bm.b], writes=[tA.b])
                        S.op("dve", TT(tB[:], pi[:, :], Bm[:, 1, cs_], ALU.mult), reads=[pi.b, Bm.b], writes=[tB.b])
                        S.op("pool", TT(Ws[:, d, 0, cs_], tA[:], tB[:], ALU.subtract), reads=[tA.b, tB.b], writes=[Ws.b])
                        S.op("dve", TT(tA[:], pr[:, :], Bm[:, 1, cs_], ALU.mult), reads=[pr.b, Bm.b, Ws.b], writes=[tA.b])
                        S.op("dve", TT(tB[:], pi[:, :], Bm[:, 0, cs_], ALU.mult), reads=[pi.b, Bm.b, Ws.b], writes=[tB.b])
                        S.op("pool", TT(Ws[:, d, 1, cs_], tA[:], tB[:], ALU.add), reads=[tA.b, tB.b], writes=[Ws.b])
            with self.scope() as es2:
                WsT = self.sb(es2, "WsT", [128, 2, 2, 4096], BF16)
                RP = self.sb(es2, "RP", [128, 2, 32, 2, 128], BF16)
                tm1 = self.sb(es2, "tm1", [128, 1024]); tm2 = self.sb(es2, "tm2", [128, 1024])
                msk = self.sb(es2, "msk", [128, 2, 128]); dcol = self.sb(es2, "dcol", [128, 64])
                S.dma("sp", msk[:], dr["masks"][:, :, :], writes=[msk.b])
                S.dma("sp", dcol[:], dr["s5_dcol"][:, l], writes=[dcol.b])
                for d in range(2):
                    for ri in range(2):
                        for c in range(4):
                            ps = self.psum()
                            pv = ps[:, :].bitcast(BF16)
                            for k in range(8):
                                q = c * 8 + k
                                S.op("pe", TR(pv[:, k * 128:(k + 1) * 128], Ws[:, d, ri, q * 128:(q + 1) * 128], self.identb[:]),
                                     reads=[Ws.b, self.identb.b], writes=[ps.b])
                            S.op("act", ACT(WsT[:, d, ri, c * 1024:(c + 1) * 1024], pv[:, 0:1024], AF.Copy), reads=[ps.b], writes=[WsT.b])
                self.s5_rc(RP, cc, slice(0, 8), slice(7, None, -1), tm1, tm2)
                for g in range(64):
                    pair, g2 = g // 2, g % 2
                    rows = slice(g2 * 64, (g2 + 1) * 64)
                    ps = self.psum()
                    for d in range(2):
                        for ri in range(2):
                            S.op("pe", MM(ps[:, d * 128:(d + 1) * 128], WsT[rows, d, ri, pair * 128:(pair + 1) * 128],
                                          RP[rows, d, pair, ri, :], ri == 0, ri == 1), reads=[WsT.b, RP.b], writes=[ps.b])
                    ta = tm1[:, (g % 2) * 256:(g % 2) * 256 + 128]; tb = tm1[:, (g % 2) * 256 + 128:(g % 2) * 256 + 256]
                    S.op("dve", TT(ta, ps[:, 0:128], msk[:, 0, :], ALU.mult), reads=[ps.b, msk.b], writes=[tm1.b])
                    S.op("dve", TT(tb, ps[:, 128:256], msk[:, 1, :], ALU.mult), reads=[ps.b, msk.b], writes=[tm1.b])
                    S.op("dve", TT(ta, ta, tb, ALU.add), reads=[tm1.b], writes=[tm1.b])
                    S.op("dve", STT(M[:, g, :], self.ident[:], dcol[:, g:g + 1], ta, ALU.mult, ALU.add),
                         reads=[tm1.b, dcol.b, self.ident.b], writes=[M.b])
            with self.scope() as es2:
                Psel = self.sb(es2, "Psel", [128, 64, 128], BF16)
                S.op("pool", lambda e: e.memset(Psel[:], 0.0), writes=[Psel.b])
                for gl in range(8):
                    for s_ in range(8):
                        S.op("dve" if s_ % 2 else "pool", CP(Psel[:, gl * 8 + s_, s_ * 16:(s_ + 1) * 16], self.identb[:, gl * 16:(gl + 1) * 16]),
                             reads=[self.identb.b], writes=[Psel.b])
                for c in range(8):
                    ps = self.psum()
                    pv = ps[:, :].bitcast(BF16)
                    for k in range(8):
                        S.op("pe", TR(pv[:, k * 128:(k + 1) * 128], Psel[:, c * 8 + k, :], self.identb[:]), reads=[Psel.b, self.identb.b], writes=[ps.b])
                    S.op("act", ACT(PselT[:, c * 8:(c + 1) * 8, :], pv[:, 0:1024].rearrange("p (k c) -> p k c", c=128), AF.Copy),
                         reads=[ps.b], writes=[PselT.b])
                for i in range(8):
                    pl = [self.psum() for _ in range(4)]
                    pc = self.psum()
                    for gl in range(8):
                        for s_ in range(8):
                            S.op("pe", MM(pl[gl // 2][:, (gl % 2) * 256:(gl % 2) * 256 + 256], Psel[:, gl * 8 + s_, :], u[:, i, TC + s_:NT:8],
                                          s_ == 0, s_ == 7), reads=[Psel.b, ub[i]], writes=[pl[gl // 2].b], inc=(s_ == 7))
                        for s_ in range(8):
                            S.op("pe", MM(pc[:, gl * 32:gl * 32 + 32], Psel[:, gl * 8 + s_, :], u[:, i, s_:TC:8],
                                          s_ == 0, s_ == 7), reads=[Psel.b, ub[i]], writes=[pc.b], inc=(s_ == 7))
                    uv = u[:, i, :].rearrange("p (g b) -> p g b", b=NB)
                    S.op("act", ACT(uv[:, :, 0:32], pc[:, 0:256].rearrange("p (g b) -> p g b", b=32), AF.Copy), reads=[pc.b], writes=[ub[i]])
                    for k in range(4):
                        S.op("act", ACT(uv[:, 2 * k:2 * k + 2, 32:NB], pl[k][:, :].rearrange("p (g b) -> p g b", b=256), AF.Copy),
                             reads=[pl[k].b], writes=[ub[i]])
            VX = self.sb(es, "VX", [128, 2, 2, 32, NB], BF16)
            for d in range(2):
                for pair in range(32):
                    for ri in range(2):
                        ps = self.psum()
                        for g2 in range(2):
                            g = 2 * pair + g2
                            S.op("pe", MM(ps[g2 * 64:(g2 + 1) * 64, 0:NB], Ws[:, d, ri, pair * 128 + g2 * 64:pair * 128 + g2 * 64 + 64], U(g),
                                          True, True), reads=[Ws.b, ub[g // 8]], writes=[ps.b])
                        if d == 0:
                            S.op("act", ACT(VX[:, d, ri, pair, :], ps[:, 0:NB], AF.Copy), reads=[ps.b], writes=[VX.b])
                        else:
                            S.op("act", ACT(VX[:, d, ri, pair, 0:32], ps[:, 31::-1], AF.Copy), reads=[ps.b], writes=[VX.b])
                            S.op("act", ACT(VX[:, d, ri, pair, 32:NB], ps[:, NB - 1:31:-1], AF.Copy), reads=[ps.b], writes=[VX.b])
            S.barrier()
            self.A.hi = himark
            S32 = self.sb(es, "S32", [128, 2, 2, 2, 32])
            Sctx = self.sb(es, "Sctx", [128, 2, 2, 32]); Fend = self.sb(es, "Fend", [128, 2, 2, 32])
            c1 = self.sb(es, "c1", [128, 2, 2, 32]); c2 = self.sb(es, "c2", [128, 2, 2, 32])
            cbuf = Buf("chain")
            for v in (S32, Sctx, Fend, c1, c2):
                v.b = cbuf
            rdc = [cbuf, VX.b, self.s5cb]
            st = {"k": 0}

            def step(k, hist):
                cur = S32[:, st["k"] % 2]
                nxt = S32[:, (st["k"] + 1) % 2]
                st["k"] += 1
                S.op("dve", TT(c1[:], cur, self.cP[:], ALU.mult), reads=rdc, writes=[cbuf])
                S.op("dve", TT(c2[:], cur[:, :, ::-1, :], self.cQ[:], ALU.mult), reads=rdc, writes=[cbuf])
                S.op("dve", TT(c1[:], c1[:], c2[:], ALU.add), reads=rdc, writes=[cbuf])
                S.op("dve", TT(nxt, c1[:], VX[:, :, :, :, k], ALU.add), reads=rdc, writes=[cbuf])
                if hist:
                    S.op("dve", CP(VX[:, :, :, :, k], cur), reads=rdc, writes=[cbuf, VX.b])

            S.op("dve", lambda e: e.memset(S32[:, 0], 0.0), reads=rdc, writes=[cbuf])
            for k in range(32):
                step(k, True)
            S.op("dve", CP(Sctx[:], S32[:, st["k"] % 2]), reads=rdc, writes=[cbuf])
            S.op("dve", lambda e: e.memset(S32[:, st["k"] % 2], 0.0), reads=rdc, writes=[cbuf])
            for k in range(32, NB):
                step(k, False)
            S.op("dve", CP(Fend[:], S32[:, st["k"] % 2]), reads=rdc, writes=[cbuf])
            s5loc, s5full = dr["s5loc"], dr["s5full"]
            S.dma("sp", s5loc[:, :], Fend[:].rearrange("p d r q -> p (d r q)"), reads=[cbuf], writes=[s5loc.b], stream="st")
            S.allgather(s5loc.t, s5full.t, [s5loc.b], [s5full.b], self.dummy[:])
            Fall = self.sb(es, "Fall", [128, 8, 2, 2, 32]); Fall.b = cbuf
            S.dma("sp", Fall[:].rearrange("p r d i q -> p r (d i q)"), s5full[:, :].rearrange("(r p) c -> p r c", p=128),
                  reads=[s5full.b], writes=[cbuf])
            Sin = self.sb(es, "Sin", [128, 2, 2, 32]); Sin.b = cbuf
            cur = self.sb(es, "rcur", [128, 2, 2, 32]); cur.b = cbuf
            S.op("dve", lambda e: e.memset(Sin[:], 0.0), reads=rdc, writes=[cbuf])
            S.op("dve", CP(cur[:], Sctx[:]), reads=rdc, writes=[cbuf])
            for i in range(8):
                for d, r in ((0, i), (1, 7 - i)):
                    S.op("dve", STT(Sin[:, d], cur[:, d], self.rank[:, r:r + 1], Sin[:, d], ALU.mult, ALU.add),
                         reads=rdc + [self.rank.b], writes=[cbuf])
                S.op("dve", TT(c1[:], cur[:], self.rP[:], ALU.mult), reads=rdc, writes=[cbuf])
                S.op("dve", TT(c2[:], cur[:, :, ::-1, :], self.rQ[:], ALU.mult), reads=rdc, writes=[cbuf])
                S.op("dve", TT(c1[:], c1[:], c2[:], ALU.add), reads=rdc, writes=[cbuf])
                S.op("dve", TT(cur[:, 0], c1[:, 0], Fall[:, i, 0], ALU.add), reads=rdc, writes=[cbuf])
                S.op("dve", TT(cur[:, 1], c1[:, 1], Fall[:, 7 - i, 1], ALU.add), reads=rdc, writes=[cbuf])
            S.op("dve", CP(S32[:, st["k"] % 2], Sin[:]), reads=rdc, writes=[cbuf])
            for k in range(32, NB):
                step(k, True)
            R = self.sb(es, "s5R", [128, 2, 16, 2, 128], BF16)
            rtm1 = self.sb(es, "rtm1", [128, 1024]); rtm2 = self.sb(es, "rtm2", [128, 1024])
            Yb = [self.sb(es, "Yb%d" % k, [128, NB], BF16) for k in range(8)]
            segs = [(32, NB)] if last else [(0, 32), (32, NB)]
            for i in range(8):
                if i % 4 == 0:
                    self.s5_rc(R, cc, slice(8, 16), slice(15, 7, -1), rtm1, rtm2, q_off=(i // 4) * 16)
                for gl in range(8):
                    g = 8 * i + gl
                    pair, g2 = g // 2, g % 2
                    rows = slice(g2 * 64, (g2 + 1) * 64)
                    ps = self.psum()
                    for (b0, b1) in segs:
                        S.op("pe", MM(ps[:, b0:b1], M[:, g, :], U(g)[:, b0:b1], True, False), reads=[M.b, ub[i]], writes=[ps.b], inc=False)
                        for d in range(2):
                            for ri in range(2):
                                if d == 0:
                                    rhs = VX[rows, d, ri, pair, b0:b1]
                                else:
                                    rhs = VX[rows, d, ri, pair, 31::-1] if b0 == 0 else VX[rows, d, ri, pair, NB - 1:31:-1]
                                S.op("pe", MM(ps[:, b0:b1], R[rows, d, pair % 16, ri, :], rhs, False, d == 1 and ri == 1),
                                     reads=[R.b, VX.b], writes=[ps.b], inc=(d == 1 and ri == 1))
                    S.op("act", ACT(Yb[gl][:, segs[0][0]:NB], ps[:, segs[0][0]:NB], AF.Copy), reads=[ps.b], writes=[Yb[gl].b])
                for t in range(8):
                    ps = self.psum()
                    b0 = segs[0][0]
                    for gl in range(8):
                        S.op("pe", MM(ps[:, b0:NB], PselT[:, gl * 8 + t, :], Yb[gl][:, b0:NB], gl == 0, gl == 7),
                             reads=[PselT.b, Yb[gl].b], writes=[ps.b], inc=(gl == 7))
                    if not last:
                        S.op("act", ACT(u[:, i, t:TC:8], ps[:, 0:32], AF.Gelu), reads=[ps.b], writes=[ub[i]])
                    S.op("act", ACT(u[:, i, TC + t:NT:8], ps[:, 32:NB], AF.Gelu), reads=[ps.b], writes=[ub[i]])
            u.b.w = None
            u.b.r = {}

    def ssd(self, l):
        nc, S, dr = self.nc, self.S, self.dram
        last = (l == DEPTH - 1)
        NCH = NT // 128
        with self.scope() as es:
            da = self.sb(es, "da", [128, 4])
            S.dma("sp", da[:], dr["dtb_alog"][:, l], writes=[da.b])
            biasw = self.sb(es, "biasw", [128, NT])
            EendB = self.sb(es, "EendB", [128, NCH, 64])
            dcb = self.sb(es, "dcb", [128, 8, 64])
            tri = self.sb(es, "tri", [128, 2, 128])
            S.dma("sp", tri[:], dr["tri"][:, :, :], writes=[tri.b])
            dvec = self.sb(es, "dvec", [128, 16])
            S.dma("sp", dvec[:], dr["ssd_dvec"][:, l], writes=[dvec.b])
            cum_d, dec_d = dr["cum_d"], dr["dec_d"]
            with self.scope() as es2:
                pb = Buf("ssdprep")

                def al(name, shape=(128, NT)):
                    v = self.sb(es2, name, list(shape))
                    v.b = pb
                    return v
                rd = [pb, da.b]
                dtraw = al("dtraw")
                S.dma("sp", dtraw[:], dr["dt_d"][:, :], reads=[dr["dt_d"].b], writes=[pb])
                dv = lambda fn, eng="dve": S.op(eng, fn, reads=rd, writes=[pb])
                xb = al("xb"); t1 = al("t1"); dt = al("dt"); lndt = al("lndt"); la = al("la")
                rst = al("rst"); cumf = al("cumf"); cumb = al("cumb"); acol = al("acol", (128, 1))
                dv(TS(xb[:], dtraw[:], da[:, 0:1], None, ALU.add))
                dv(STT(t1[:], xb[:], -1.0, xb[:], ALU.mult, ALU.max))
                dv(ACT(t1[:], t1[:], AF.Exp, scale=-1.0), "act")
                dv(ACT(t1[:], t1[:], AF.Ln, bias=1.0), "act")
                dv(STT(dt[:], xb[:], 0.0, t1[:], ALU.max, ALU.add))
                dv(ACT(lndt[:], dt[:], AF.Ln), "act")
                dv(ACT(acol[:], da[:, 1:2], AF.Exp), "act")
                dv(TS(acol[:], acol[:], -1.0, None, ALU.mult))
                dv(TS(la[:], dt[:], acol[:, 0:1], None, ALU.mult))
                dv(lambda e: e.memset(rst[:], 1.0))
                dv(lambda e: e.memset(rst[:, 0:NT:128], 0.0))
                dv(lambda e: e.tensor_tensor_scan(out=cumf[:], data0=rst[:], data1=la[:], initial=0.0, op0=ALU.mult, op1=ALU.add))
                dv(lambda e: e.memset(rst[:], 1.0))
                dv(lambda e: e.memset(rst[:, 127:NT:128], 0.0))
                dv(lambda e: e.tensor_tensor_scan(out=cumb[:, ::-1], data0=rst[:, ::-1], data1=la[:, ::-1], initial=0.0,
                                                  op0=ALU.mult, op1=ALU.add))
                cum = xb
                dv(TS(cum[:], cumf[:], da[:, 2:3], None, ALU.mult))
                dv(STT(cum[:], cumb[:], da[:, 3:4], cum[:], ALU.mult, ALU.add))
                tot = al("tot", (128, NCH)); t18 = al("t18", (128, NCH)); pre = al("pre", (128, NCH)); one18 = al("one18", (128, NCH))
                dv(CP(tot[:], cumf[:, 127:NT:128]))
                dv(TT(biasw[:], lndt[:], cum[:], ALU.subtract))
                S.dma("sp", cum_d[:, :], cum[0:64, :], reads=[pb], writes=[cum_d.b], stream="st")
                dv(ACT(t1[:], cum[:], AF.Exp), "act")
                S.dma("sp", dec_d[:, :], t1[0:64, :], reads=[pb], writes=[dec_d.b], stream="st")
                S.op("dve", TT(biasw[64:128, :].rearrange("p (c t) -> p c t", t=128), biasw[64:128, :].rearrange("p (c t) -> p c t", t=128),
                               tot[64:128, :].unsqueeze(2).to_broadcast([64, NCH, 128]), ALU.add), reads=rd + [biasw.b], writes=[biasw.b])
                S.op("act", ACT(biasw[64:128, :], biasw[64:128, :], AF.Exp), reads=[biasw.b], writes=[biasw.b])
                dv(ACT(t18[:], tot[:], AF.Exp), "act")
                dg = al("dg", (128, NCH, 64))
                S.op("dve", TT(dg[0:64], self.ident[0:64, 0:64].unsqueeze(1).to_broadcast([64, NCH, 64]),
                               t18[0:64, :].unsqueeze(2).to_broadcast([64, NCH, 64]), ALU.mult), reads=rd + [self.ident.b], writes=[pb])
                ones = al("ones", (128, 128))
                dv(lambda e: e.memset(ones[:], 1.0))
                dgf = dg[0:64].rearrange("p c j -> p (c j)")
                ebf = EendB[:].rearrange("p c j -> p (c j)")
                for c0 in range(0, NCH * 64, 512):
                    n = min(512, NCH * 64 - c0)
                    ps = self.psum()
                    S.op("pe", MM(ps[:, :n], ones[0:64, :], dgf[:, c0:c0 + n], True, True), reads=[pb], writes=[ps.b])
                    S.op("act", ACT(ebf[:, c0:c0 + n], ps[:, :n], AF.Copy), reads=[ps.b], writes=[EendB.b])
                dv(lambda e: e.memset(one18[:], 1.0))
                dv(lambda e: e.tensor_tensor_scan(out=pre[:, 0:16], data0=one18[:, 0:16], data1=tot[:, 2:NCH], initial=0.0, op0=ALU.mult, op1=ALU.add))
                dv(ACT(t18[:, 0:1], pre[:, 15:16], AF.Exp), "act")
                S.dma("sp", dr["dcloc"][:, :], t18[0:64, 0:1], reads=[pb], writes=[dr["dcloc"].b], stream="st")
            xT_d, bcT_d = dr["xT_d"], dr["bcT_d"]
            xTc = [self.sb(es, "xTc%d" % i, [128, 16, 128], BF16) for i in range(2)]
            bcc = [self.sb(es, "bcc%d" % i, [128, 8, 128], BF16) for i in range(2)]
            xtok = [self.sb(es, "xtok%d" % i, [128, 2048], BF16) for i in range(2)]
            btok = [self.sb(es, "btok%d" % i, [128, 4, 128], BF16) for i in range(2)]
            bwT = [self.sb(es, "bwT%d" % i, [128, 128]) for i in range(2)]
            cnt = {"n": 0}

            def load_chunk(c, need_b):
                k = cnt["n"] % 2
                cnt["n"] += 1
                cs = slice(c * 128, (c + 1) * 128)
                S.dma("sp", xTc[k][:], xT_d[:, cs].rearrange("(i p) t -> p i t", p=128), reads=[xT_d.b], writes=[xTc[k].b], stream="ldc")
                S.dma("sp", bcc[k][:], bcT_d[:, cs].rearrange("(i p) t -> p i t", p=128), reads=[bcT_d.b], writes=[bcc[k].b], stream="ldc")
                for half in range(2):
                    ps = self.psum()
                    pv = ps[:, :].bitcast(BF16)
                    for j in range(8):
                        S.op("pe", TR(pv[:, j * 128:(j + 1) * 128], xTc[k][:, half * 8 + j, :], self.identb[:]),
                             reads=[xTc[k].b, self.identb.b], writes=[ps.b])
                    S.op("act", ACT(xtok[k][:, half * 1024:(half + 1) * 1024], pv[:, 0:1024], AF.Copy), reads=[ps.b], writes=[xtok[k].b])
                if need_b:
                    ps = self.psum()
                    pv = ps[:, :].bitcast(BF16)
                    for g in range(4):
                        S.op("pe", TR(pv[:, g * 128:(g + 1) * 128], bcc[k][:, g, :], self.identb[:]), reads=[bcc[k].b, self.identb.b], writes=[ps.b])
                    S.op("act", ACT(btok[k][:].rearrange("p g n -> p (g n)"), pv[:, 0:512], AF.Copy), reads=[ps.b], writes=[btok[k].b])
                ps = self.psum()
                S.op("pe", TR(ps[:, 0:128], biasw[:, cs], self.ident[:]), reads=[biasw.b, self.ident.b], writes=[ps.b])
                S.op("act", ACT(bwT[k][:], ps[:, 0:128], AF.Copy), reads=[ps.b], writes=[bwT[k].b])
                return k

            sloc_d, sin_d = dr["sloc_d"], dr["sin_d"]
            with self.scope() as es2:
                xw = [self.sb(es2, "xw%d" % i, [128, 2048], BF16) for i in range(2)]
                slb = [self.sb(es2, "slb%d" % i, [128, 2, 2048]) for i in range(2)]
                for c in range(NCH):
                    k = load_chunk(c, True)
                    sb_ = slb[c % 2]
                    for kk in range(2):
                        S.op("dve" if kk == 0 else "pool",
                             TT(xw[kk][:].rearrange("p (h q) -> p h q", q=64), xtok[k][:].rearrange("p (h q) -> p h q", q=64),
                                bwT[k][:, 64 + kk * 32:64 + kk * 32 + 32].unsqueeze(2).to_broadcast([128, 32, 64]), ALU.mult),
                             reads=[xtok[k].b, bwT[k].b], writes=[xw[kk].b])
                        for g in range(4):
                            ps = self.psum()
                            S.op("pe", MM(ps[:, :], btok[k][:, g, :], xw[kk][:, g * 512:(g + 1) * 512], True, True),
                                 reads=[btok[k].b, xw[kk].b], writes=[ps.b])
                            S.op("act", ACT(sb_[:, kk, g * 512:(g + 1) * 512], ps[:, :], AF.Copy), reads=[ps.b], writes=[sb_.b])
                    S.dma("sp", sloc_d[c], sb_[:].rearrange("p k f -> p (k f)"), reads=[sb_.b], writes=[sloc_d.b], stream="st")
            with self.scope() as es2:
                St = self.sb(es2, "St", [128, 2, 2048]); Sl = [self.sb(es2, "Sl%d" % i, [128, 2, 2048]) for i in range(2)]
                Sb16 = [self.sb(es2, "Sb16%d" % i, [128, 2048], BF16) for i in range(2)]
                Sctx = self.sb(es2, "SctxS", [128, 2, 2048])
                cbf = Buf("ssdchain")
                St.b = cbf; Sctx.b = cbf
                nl = {"n": 0}

                def chain(kk, order, store, init):
                    Sk = St[:, kk, :]
                    if init is None:
                        S.op("dve", lambda e: e.memset(Sk, 0.0), reads=[cbf], writes=[cbf])
                    for c in order:
                        slt = Sl[nl["n"] % 2]
                        sb16 = Sb16[nl["n"] % 2]
                        nl["n"] += 1
                        S.dma("sp", slt[:, kk, :], sloc_d[c][:, kk * 2048:(kk + 1) * 2048], reads=[sloc_d.b], writes=[slt.b], stream="ldc")
                        if store:
                            S.op("act", ACT(sb16[:], Sk, AF.Copy), reads=[cbf], writes=[sb16.b])
                            S.dma("sp", sin_d[c][:, kk * 2048:(kk + 1) * 2048], sb16[:], reads=[sb16.b], writes=[sin_d.b], stream="st")
                        eb = EendB[:, c, kk * 32:(kk + 1) * 32].unsqueeze(2).to_broadcast([128, 32, 64])
                        S.op("dve", TT(Sk.rearrange("p (h q) -> p h q", q=64), Sk.rearrange("p (h q) -> p h q", q=64), eb, ALU.mult),
                             reads=[cbf, EendB.b], writes=[cbf])
                        S.op("dve", TT(Sk, Sk, slt[:, kk, :], ALU.add), reads=[cbf, slt.b], writes=[cbf])

                chain(0, [0, 1], True, None)
                chain(1, [1, 0], True, None)
                S.op("dve", CP(Sctx[:], St[:]), reads=[cbf], writes=[cbf])
                chain(0, list(range(2, NCH)), False, None)
                chain(1, list(range(NCH - 1, 1, -1)), False, None)
                ssdloc, ssdfull = dr["ssdloc"], dr["ssdfull"]
                S.dma("sp", ssdloc[:, :], St[:].rearrange("p k f -> p (k f)"), reads=[cbf], writes=[ssdloc.b], stream="st")
                S.allgather(ssdloc.t, ssdfull.t, [ssdloc.b], [ssdfull.b], self.dummy[:])
                S.allgather(dr["dcloc"].t, dr["dcfull"].t, [dr["dcloc"].b], [dr["dcfull"].b], self.dummy[:])
                S.dma("sp", dcb[:].rearrange("p r j -> p (r j)"), dr["dcfull"][:, :].rearrange("r o -> (o r)").partition_broadcast(128),
                      reads=[dr["dcfull"].b], writes=[dcb.b])
                Sin = Sctx
                acc = self.sb(es2, "relacc", [128, 2, 2048]); acc.b = cbf
                S.op("dve", lambda e: e.memset(acc[:], 0.0), reads=[cbf], writes=[cbf])
                S.op("dve", CP(St[:], Sctx[:]), reads=[cbf], writes=[cbf])
                for i in range(8):
                    for kk, r in ((0, i), (1, 7 - i)):
                        slt = Sl[nl["n"] % 2]
                        nl["n"] += 1
                        S.dma("sp", slt[:, kk, :], ssdfull[r * 128:(r + 1) * 128, kk * 2048:(kk + 1) * 2048], reads=[ssdfull.b], writes=[slt.b], stream="ldc")
                        Sk = St[:, kk, :]
                        S.op("dve", STT(acc[:, kk, :], Sk, self.rank[:, r:r + 1], acc[:, kk, :], ALU.mult, ALU.add), reads=[cbf, self.rank.b], writes=[cbf])
                        db = dcb[:, r, kk * 32:(kk + 1) * 32].unsqueeze(2).to_broadcast([128, 32, 64])
                        S.op("dve", TT(Sk.rearrange("p (h q) -> p h q", q=64), Sk.rearrange("p (h q) -> p h q", q=64), db, ALU.mult),
                             reads=[cbf, dcb.b], writes=[cbf])
                        S.op("dve", TT(Sk, Sk, slt[:, kk, :], ALU.add), reads=[cbf, slt.b], writes=[cbf])
                S.op("dve", CP(St[:], acc[:]), reads=[cbf], writes=[cbf])
                chain(0, list(range(2, NCH)), True, "keep")
                chain(1, list(range(NCH - 1, 1, -1)), True, "keep")
            with self.scope() as es2:
                cumB = [self.sb(es2, "cumB%d" % i, [128, 64, 128]) for i in range(1)]
                dec2 = [self.sb(es2, "dec2%d" % i, [128, 2, 16, 128]) for i in range(1)]
                sinb = [self.sb(es2, "sinb%d" % i, [128, 2, 2048], BF16) for i in range(2)]
                CBm = self.sb(es2, "CBm", [128, 4, 2, 128])
                Lt = [self.sb(es2, "Lt%d" % i, [128, 128]) for i in range(3)]
                sc = [self.sb(es2, "sc%d" % i, [128, 128], BF16) for i in range(3)]
                yacc = [self.sb(es2, "yacc%d" % i, [128, 16, 128]) for i in range(2)]
                tmpy = self.sb(es2, "tmpy", [128, 512])
                yssd_d = dr["yssd_d"]
                chunks = list(range(2, NCH)) if last else list(range(NCH))
                nn = 0
                for ci, c in enumerate(chunks):
                    cs = slice(c * 128, (c + 1) * 128)
                    k = load_chunk(c, False)
                    cb_, d2, sn, ya = cumB[0], dec2[0], sinb[ci % 2], yacc[ci % 2]
                    S.dma("sp", cb_[:], cum_d[:, cs].partition_broadcast(128), reads=[cum_d.b], writes=[cb_.b], stream="ldc")
                    for h2 in range(2):
                        S.dma("sp", d2[h2 * 64:(h2 + 1) * 64], dec_d[:, cs].rearrange("(k i h) t -> h k i t", k=2, h=2)[h2].partition_broadcast(64),
                              reads=[dec_d.b], writes=[d2.b], stream="ldc")
                    S.dma("sp", sn[:].rearrange("p k f -> p (k f)"), sin_d[c], reads=[sin_d.b], writes=[sn.b], stream="ldc")
                    ps = self.psum()
                    for g in range(4):
                        S.op("pe", MM(ps[:, g * 128:(g + 1) * 128], bcc[k][:, g, :], bcc[k][:, 4 + g, :], True, True), reads=[bcc[k].b], writes=[ps.b])
                    for kk in range(2):
                        S.op("dve", TT(CBm[:, :, kk, :], ps[:, :].rearrange("p (g t) -> p g t", t=128),
                                       tri[:, kk, :].unsqueeze(1).to_broadcast([128, 4, 128]), ALU.mult), reads=[ps.b, tri.b], writes=[CBm.b])
                    for g in range(4):
                        py = self.psum()
                        pin = [self.psum(), self.psum()]
                        for hh in range(8):
                            h = g * 8 + hh
                            orow = slice((h % 2) * 64, (h % 2) * 64 + 64)
                            ocol = slice((hh // 2) * 128, (hh // 2) * 128 + 128)
                            for kk in range(2):
                                L, s_ = Lt[nn % 3], sc[nn % 3]
                                nn += 1
                                S.op("act", ACT(L[:], cb_[:, kk * 32 + h, :], AF.Exp, bias=bwT[k][:, kk * 32 + h:kk * 32 + h + 1]),
                                     reads=[cb_.b, bwT[k].b], writes=[L.b])
                                S.op("dve", STT(s_[:], L[:], 1e30, CBm[:, g, kk, :], ALU.min, ALU.mult), reads=[L.b, CBm.b], writes=[s_.b])
                                S.op("pe", MM(py[orow, ocol], xtok[k][:, h * 64:(h + 1) * 64], s_[:], kk == 0, kk == 1),
                                     reads=[xtok[k].b, s_.b], writes=[py.b])
                                S.op("pe", MM(pin[kk][orow, ocol], sn[:, kk, h * 64:(h + 1) * 64], bcc[k][:, 4 + g, :], True, True),
                                     reads=[sn.b, bcc[k].b], writes=[pin[kk].b])
                        yv = ya[:, g * 4:(g + 1) * 4, :]
                        S.op("act", ACT(yv, py[:, :].rearrange("p (i t) -> p i t", t=128), AF.Copy), reads=[py.b], writes=[ya.b])
                        for kk in range(2):
                            S.op("dve", TT(tmpy[:].rearrange("p (i t) -> p i t", t=128), pin[kk][:, :].rearrange("p (i t) -> p i t", t=128),
                                           d2[:, kk, g * 4:(g + 1) * 4, :], ALU.mult), reads=[pin[kk].b, d2.b], writes=[tmpy.b])
                            S.op("pool", TT(yv, yv, tmpy[:].rearrange("p (i t) -> p i t", t=128), ALU.add), reads=[tmpy.b, ya.b], writes=[ya.b])
                    for i in range(16):
                        S.op("dve", STT(ya[:, i, :], xTc[k][:, i, :], dvec[:, i:i + 1], ya[:, i, :], ALU.mult, ALU.add),
                             reads=[xTc[k].b, dvec.b, ya.b], writes=[ya.b])
                    S.dma("sp", yssd_d[:, cs].rearrange("(i p) t -> p i t", p=128), ya[:], reads=[ya.b], writes=[yssd_d.b], stream="st")

    def wtile(self, wb, w_dr, c0, kt, ncols=128):
        self.S.dma("pool", wb[:, 0:kt, 0:ncols], w_dr[:, c0:c0 + ncols].rearrange("(kc p) c -> p kc c", p=128),
                   reads=[w_dr.b], writes=[wb.b], stream="ldw")

    def ln_alloc(self, es):
        o = self.sb(es, "ln_o", [128, 16, 512]); xb = self.sb(es, "ln_xb", [128, 16, 512], BF16)
        sq = self.sb(es, "ln_sq", [128, 16, 512], BF16)
        mean = self.sb(es, "ln_m", [128, 512]); rstd = self.sb(es, "ln_r", [128, 512]); nb = self.sb(es, "ln_nb", [128, 512])
        ob = self.sb(es, "ln_ob", [128, 16, 512], BF16)
        return (o, xb, sq, mean, rstd, nb, ob)

    def layernorm(self, bufs, l, which, src_d, cols, n, dsts):
        S, dr = self.S, self.dram
        o, xb, sq, mean, rstd, nb, ob = bufs
        S.dma("sp", o[:, :, :n], src_d[:, cols].rearrange("(kc p) t -> p kc t", p=128), reads=[src_d.b], writes=[o.b], stream="lda")
        S.op("act", ACT(xb[:, :, :n], o[:, :, :n], AF.Copy), reads=[o.b], writes=[xb.b])
        S.op("act", ACT(sq[:, :, :n], o[:, :, :n], AF.Square), reads=[o.b], writes=[sq.b])
        p1, p2 = self.psum(), self.psum()
        for kc in range(16):
            S.op("pe", MM(p1[:, :n], self.onesb[:], xb[:, kc, :n], kc == 0, kc == 15), reads=[xb.b, self.onesb.b], writes=[p1.b], inc=(kc == 15))
        for kc in range(16):
            S.op("pe", MM(p2[:, :n], self.onesb[:], sq[:, kc, :n], kc == 0, kc == 15), reads=[sq.b, self.onesb.b], writes=[p2.b], inc=(kc == 15))
        S.op("dve", TS(mean[:, :n], p1[:, :n], 1.0 / D, None, ALU.mult), reads=[p1.b], writes=[mean.b])
        S.op("dve", TT(nb[:, :n], mean[:, :n], mean[:, :n], ALU.mult), reads=[mean.b], writes=[nb.b])
        S.op("dve", STT(rstd[:, :n], p2[:, :n], 1.0 / D, nb[:, :n], ALU.mult, ALU.subtract), reads=[p2.b, nb.b], writes=[rstd.b])
        S.op("dve", TS(rstd[:, :n], rstd[:, :n], LN_EPS, None, ALU.add), reads=[rstd.b], writes=[rstd.b])
        S.op("act", ACT(rstd[:, :n], rstd[:, :n], AF.Sqrt), reads=[rstd.b], writes=[rstd.b])
        S.op("dve", lambda e: e.reciprocal(rstd[:, :n], rstd[:, :n]), reads=[rstd.b], writes=[rstd.b])
        S.op("dve", STT(nb[:, :n], mean[:, :n], -1.0, rstd[:, :n], ALU.mult, ALU.mult), reads=[mean.b, rstd.b], writes=[nb.b])
        gi = 0 if which == 1 else 2
        for kc in range(16):
            eng = "dve" if kc % 2 == 0 else "pool"
            S.op(eng, TT(o[:, kc, :n], o[:, kc, :n], rstd[:, :n], ALU.mult), reads=[o.b, rstd.b], writes=[o.b])
            S.op(eng, TT(o[:, kc, :n], o[:, kc, :n], nb[:, :n], ALU.add), reads=[o.b, nb.b], writes=[o.b])
            S.op(eng, TS(o[:, kc, :n], o[:, kc, :n], self.lngb[:, l, gi, kc:kc + 1], self.lngb[:, l, gi + 1, kc:kc + 1], ALU.mult, ALU.add),
                 reads=[o.b, self.lngb.b], writes=[o.b])
        for (dst, c0, kind) in dsts:
            if kind == "h":
                S.dma("sp", dst[:, c0:c0 + n].rearrange("(kc p) t -> p kc t", p=128), o[:, :, :n], reads=[o.b], writes=[dst.b], stream="st")
            else:
                _, j = kind
                for kc in range(16):
                    eng = "dve" if kc % 2 == 0 else "pool"
                    S.op(eng, TS(ob[:, kc, :n], o[:, kc, :n], self.ops[:, l, 1, kc, j:j + 1], self.modv(3, kc, j), ALU.mult, ALU.add),
                         reads=[o.b, self.ops.b, self.mod.b], writes=[ob.b])
                S.dma("sp", dst[:, c0:c0 + n].rearrange("(kc p) t -> p kc t", p=128), ob[:, :, :n], reads=[ob.b], writes=[dst.b], stream="st")

    def phaseC(self, l):
        nc, S, dr = self.nc, self.S, self.dram
        last = (l == DEPTH - 1)
        g5 = self.u
        res_lat = dr["x_fm"] if l == 0 else dr["hlat_d"]
        res_ctx = dr["ctx_fm"] if l == 0 else dr["hctx_d"]
        sbs = [(TC if last else 0, TC + 1024), (TC + 1024, NT)]
        for nm in ("w_glu", "w_s5p", "w_ssdp", "w_out"):
            self.gather_wait("%s%d" % (nm, l))
        wglu, ws5p, wssdp, wout = (dr["%s%d" % (nm, l)] for nm in ("w_glu", "w_s5p", "w_ssdp", "w_out"))
        nw = self.nwt
        S.dma("sp", nw[:], dr["ssd_nw"][:, l], writes=[nw.b])
        for (s0, s1) in sbs:
            N = s1 - s0
            blks = []
            b = s0
            while b < s1:
                e = min(s1, (TC if b < TC else b + 512))
                blks.append((b, e - b))
                b = e
            with self.scope() as es:
                merged = self.sb(es, "merged", [128, 16, N], BF16)
                with self.scope() as es2:
                    glu = self.sb(es2, "glu", [128, 8, N], BF16)
                    hn = self.sb(es2, "hn", [128, 16, N], BF16)
                    wA = [self.sb(es2, "wA%d" % i, [128, 16, 128], BF16) for i in range(2)]
                    wB = [self.sb(es2, "wB%d" % i, [128, 16, 128], BF16) for i in range(2)]
                    sg = [self.sb(es2, "sg%d" % i, [128, 512]) for i in range(2)]
                    for i in range(8):
                        a, bw = wA[i % 2], wB[i % 2]
                        self.wtile(a, wglu, i * 128, 8); self.wtile(bw, wglu, S5W + i * 128, 8)
                        for bi, (b0, n) in enumerate(blks):
                            pa, pb = self.psum(), self.psum()
                            for kc in range(8):
                                S.op("pe", MM(pa[:, :n], a[:, kc, :], g5[:, kc, b0:b0 + n], kc == 0, kc == 7), reads=[a.b, g5.b], writes=[pa.b], inc=(kc == 7))
                            for kc in range(8):
                                S.op("pe", MM(pb[:, :n], bw[:, kc, :], g5[:, kc, b0:b0 + n], kc == 0, kc == 7), reads=[bw.b, g5.b], writes=[pb.b], inc=(kc == 7))
                            s_ = sg[bi % 2]
                            S.op("act", ACT(s_[:, :n], pb[:, :n], AF.Sigmoid), reads=[pb.b], writes=[s_.b])
                            S.op("dve", TT(glu[:, i, b0 - s0:b0 - s0 + n], pa[:, :n], s_[:, :n], ALU.mult), reads=[pa.b, s_.b], writes=[glu.b])
                    if "glu_d" in self.dbg:
                        S.dma("sp", dr["glu_d"][:, s0:s1].rearrange("(i p) t -> p i t", p=128), glu[:], reads=[glu.b], writes=[dr["glu_d"].b], stream="st")
                    with self.scope() as es3:
                        y4 = [self.sb(es3, "y4%d" % i, [128, 4, 512]) for i in range(2)]
                        z4 = [self.sb(es3, "z4%d" % i, [128, 4, 512]) for i in range(2)]
                        q4 = [self.sb(es3, "q4%d" % i, [128, 4, 512], BF16) for i in range(2)]
                        rs = [self.sb(es3, "rs%d" % i, [128, 512]) for i in range(2)]
                        it = 0
                        for (b0, n) in blks:
                            for g in range(4):
                                y, z, q, r_ = y4[it % 2], z4[it % 2], q4[it % 2], rs[it % 2]
                                it += 1
                                rows = slice(g * 512, (g + 1) * 512)
                                S.dma("sp", y[:, :, :n], dr["yssd_d"][rows, b0:b0 + n].rearrange("(i p) t -> p i t", p=128), reads=[dr["yssd_d"].b], writes=[y.b], stream="lda")
                                S.dma("sp", z[:, :, :n], dr["zs_d"][rows, b0:b0 + n].rearrange("(i p) t -> p i t", p=128), reads=[dr["zs_d"].b], writes=[z.b], stream="lda")
                                S.op("dve", TT(y[:, :, :n], y[:, :, :n], z[:, :, :n], ALU.mult), reads=[y.b, z.b], writes=[y.b])
                                S.op("act", ACT(q[:, :, :n], y[:, :, :n], AF.Square), reads=[y.b], writes=[q.b])
                                ps = self.psum()
                                for i in range(4):
                                    S.op("pe", MM(ps[:, :n], self.onesb[:], q[:, i, :n], i == 0, i == 3), reads=[q.b, self.onesb.b], writes=[ps.b], inc=(i == 3))
                                S.op("dve", TS(r_[:, :n], ps[:, :n], 1.0 / 512, RMS_EPS, ALU.mult, ALU.add), reads=[ps.b], writes=[r_.b])
                                S.op("act", ACT(r_[:, :n], r_[:, :n], AF.Sqrt), reads=[r_.b], writes=[r_.b])
                                S.op("dve", lambda e, r_=r_, n=n: e.reciprocal(r_[:, :n], r_[:, :n]), reads=[r_.b], writes=[r_.b])
                                for i in range(4):
                                    S.op("dve", STT(hn[:, g * 4 + i, b0 - s0:b0 - s0 + n], y[:, i, :n], nw[:, g * 4 + i:g * 4 + i + 1], r_[:, :n], ALU.mult, ALU.mult),
                                         reads=[y.b, r_.b, nw.b], writes=[hn.b])
                    if "hn_d" in self.dbg:
                        S.dma("sp", dr["hn_d"][:, s0:s1].rearrange("(i p) t -> p i t", p=128), hn[:], reads=[hn.b], writes=[dr["hn_d"].b], stream="st")
                    gA = [self.sb(es2, "gA%d" % i, [128, 512]) for i in range(2)]
                    gB = [self.sb(es2, "gB%d" % i, [128, 512]) for i in range(2)]
                    it = 0
                    for j in range(16):
                        a, bw = wA[j % 2], wB[j % 2]
                        self.wtile(a, ws5p, j * 128, 8); self.wtile(bw, wssdp, j * 128, 16)
                        for (b0, n) in blks:
                            ga, gb = gA[it % 2], gB[it % 2]
                            it += 1
                            S.dma("sp", ga[:, :n], dr["g_d"][j * 128:(j + 1) * 128, b0:b0 + n], reads=[dr["g_d"].b], writes=[ga.b], stream="lda")
                            S.dma("sp", gb[:, :n], dr["g_d"][D + j * 128:D + (j + 1) * 128, b0:b0 + n], reads=[dr["g_d"].b], writes=[gb.b], stream="lda")
                            pa, pb = self.psum(), self.psum()
                            for kc in range(8):
                                S.op("pe", MM(pa[:, :n], a[:, kc, :], glu[:, kc, b0 - s0:b0 - s0 + n], kc == 0, kc == 7), reads=[a.b, glu.b], writes=[pa.b], inc=(kc == 7))
                            for kc in range(16):
                                S.op("pe", MM(pb[:, :n], bw[:, kc, :], hn[:, kc, b0 - s0:b0 - s0 + n], kc == 0, kc == 15), reads=[bw.b, hn.b], writes=[pb.b], inc=(kc == 15))
                            S.op("dve", TT(ga[:, :n], pa[:, :n], ga[:, :n], ALU.mult), reads=[pa.b, ga.b], writes=[ga.b])
                            S.op("dve", TT(gb[:, :n], pb[:, :n], gb[:, :n], ALU.mult), reads=[pb.b, gb.b], writes=[gb.b])
                            S.op("pool", TT(merged[:, j, b0 - s0:b0 - s0 + n], ga[:, :n], gb[:, :n], ALU.add), reads=[ga.b, gb.b], writes=[merged.b])
                with self.scope() as es2:
                    wO = [self.sb(es2, "wO%d" % i, [128, 16, 128], BF16) for i in range(2)]
                    hs = [self.sb(es2, "hs%d" % i, [128, 512]) for i in range(2)]
                    it = 0
                    for j in range(16):
                        w = wO[j % 2]
                        self.wtile(w, wout, j * 128, 16)
                        for (b0, n) in blks:
                            h = hs[it % 2]
                            it += 1
                            jm = 1 if b0 < TC else 0
                            src = res_ctx[j * 128:(j + 1) * 128, b0:b0 + n] if b0 < TC else res_lat[j * 128:(j + 1) * 128, b0 - TC:b0 - TC + n]
                            S.dma("sp", h[:, :n], src, reads=[(res_ctx if b0 < TC else res_lat).b], writes=[h.b], stream="lda")
                            ps = self.psum()
                            for kc in range(16):
                                S.op("pe", MM(ps[:, :n], w[:, kc, :], merged[:, kc, b0 - s0:b0 - s0 + n], kc == 0, kc == 15), reads=[w.b, merged.b], writes=[ps.b], inc=(kc == 15))
                            S.op("act", ACT(h[:, :n], h[:, :n], AF.Copy, scale=ALPHA), reads=[h.b], writes=[h.b])
                            S.op("dve", STT(h[:, :n], ps[:, :n], self.modv(2, j, jm), h[:, :n], ALU.mult, ALU.add), reads=[ps.b, h.b, self.mod.b], writes=[h.b])
                            S.dma("sp", dr["pre_d"][j * 128:(j + 1) * 128, b0:b0 + n], h[:, :n], reads=[h.b], writes=[dr["pre_d"].b], stream="st")
            with self.scope() as es:
                lnb = self.ln_alloc(es)
                for (b0, n) in blks:
                    jm = 1 if b0 < TC else 0
                    xc0 = (2 * 64 + TL + b0) if b0 < TC else (64 + b0 - TC)
                    self.layernorm(lnb, l, 1, dr["pre_d"], slice(b0, b0 + n), n,
                                   [(dr["h1_d"], b0, "h"), (dr["xm2_d"], xc0, ("xm", jm))])

    def phaseD(self, l):
        nc, S, dr = self.nc, self.S, self.dram
        last = (l == DEPTH - 1)
        xm2_d, h1_d, pre2 = dr["xm2_d"], dr["h1_d"], dr["pre2_d"]
        XL = 64 + TL + 64
        with self.scope() as es:
            hx = self.sb(es, "hx", [128, 16, 128], BF16)
            S.dma("sp", hx[:, :, 0:64], xm2_d[:, 64:128].rearrange("(kc p) t -> p kc t", p=128), reads=[xm2_d.b], writes=[hx.b], stream="lda")
            S.dma("sp", hx[:, :, 64:128], xm2_d[:, TL:TL + 64].rearrange("(kc p) t -> p kc t", p=128), reads=[xm2_d.b], writes=[hx.b], stream="lda")
            S.dma("sp", dr["hxloc"][:, :].rearrange("(kc p) t -> p kc t", p=128), hx[:], reads=[hx.b], writes=[dr["hxloc"].b], stream="st")
            S.allgather(dr["hxloc"].t, dr["hxfull"].t, [dr["hxloc"].b], [dr["hxfull"].b], self.dummy[:])
            allr = self.sb(es, "allr", [128, 8, 16, 128], BF16)
            S.dma("sp", allr[:].rearrange("p r k t -> p (r k) t"), dr["hxfull"][:, :].rearrange("(rk p) t -> p rk t", p=128),
                  reads=[dr["hxfull"].b], writes=[allr.b], stream="lda")
            acc = self.sb(es, "hacc", [128, 2, 16, 64]); accb = self.sb(es, "haccb", [128, 2, 16, 64], BF16)
            S.op("dve", lambda e: e.memset(acc[:], 0.0), writes=[acc.b])
            for r in range(8):
                S.op("dve", STT(acc[:, 0], allr[:, r, :, 64:128], self.rank[:, 8 + r:9 + r], acc[:, 0], ALU.mult, ALU.add),
                     reads=[allr.b, self.rank.b, acc.b], writes=[acc.b])
                S.op("dve", STT(acc[:, 1], allr[:, r, :, 0:64], self.rank[:, 16 + r:17 + r], acc[:, 1], ALU.mult, ALU.add),
                     reads=[allr.b, self.rank.b, acc.b], writes=[acc.b])
            S.op("dve", CP(accb[:], acc[:]), reads=[acc.b], writes=[accb.b])
            S.dma("sp", xm2_d[:, 0:64].rearrange("(kc p) t -> p kc t", p=128), accb[:, 0], reads=[accb.b], writes=[xm2_d.b], stream="st")
            S.dma("sp", xm2_d[:, 64 + TL:XL].rearrange("(kc p) t -> p kc t", p=128), accb[:, 1], reads=[accb.b], writes=[xm2_d.b], stream="st")
        self.gather_wait("w_up%d" % l); self.gather_wait("w_down%d" % l)
        wup, wdn = dr["w_up%d" % l], dr["w_down%d" % l]
        sbs = [(0, 1024 + 128, 1024, TC, GRID_W, False), (1024, 1024 + 128, 1024, TC + 1024, GRID_W, False)]
        if not last:
            sbs.append((XL, TC, TC, 0, TC, True))
        for (xc0, WN, N, oc0, GW, isctx) in sbs:
            jm = 1 if isctx else 0
            halo = 0 if isctx else 64
            with self.scope() as es:
                xw_ = self.sb(es, "xm2w", [128, 16, WN], BF16)
                S.dma("sp", xw_[:], xm2_d[:, xc0:xc0 + WN].rearrange("(kc p) t -> p kc t", p=128), reads=[xm2_d.b], writes=[xw_.b], stream="lda")
                actb = self.sb(es, "actb", [128, FT, N], BF16)
                cwt = self.sb(es, "fcw", [128, FT, 9]); cbt = self.sb(es, "fcb", [128, FT])
                S.dma("sp", cwt[:], dr["ffn_cw"][:, l], writes=[cwt.b]); S.dma("sp", cbt[:], dr["ffn_cb"][:, l], writes=[cbt.b])
                with self.scope() as es2:
                    wg = [self.sb(es2, "wg%d" % i, [128, 16, 128], BF16) for i in range(2)]
                    wv = [self.sb(es2, "wv%d" % i, [128, 16, 128], BF16) for i in range(2)]
                    gb_ = [self.sb(es2, "gbuf%d" % i, [128, WN]) for i in range(2)]
                    vb_ = [self.sb(es2, "vbuf%d" % i, [128, N]) for i in range(2)]
                    ca = [self.sb(es2, "cacc%d" % i, [128, N]) for i in range(2)]
                    for f in range(FT):
                        g_, v_, gbf, vbf, a = wg[f % 2], wv[f % 2], gb_[f % 2], vb_[f % 2], ca[f % 2]
                        self.wtile(g_, wup, f * 128, 16); self.wtile(v_, wup, DFF + f * 128, 16)
                        for c0 in range(0, WN, 512):
                            n = min(512, WN - c0)
                            ps = self.psum()
                            for kc in range(16):
                                S.op("pe", MM(ps[:, :n], g_[:, kc, :], xw_[:, kc, c0:c0 + n], kc == 0, kc == 15), reads=[g_.b, xw_.b], writes=[ps.b], inc=(kc == 15))
                            S.op("act", ACT(gbf[:, c0:c0 + n], ps[:, :n], AF.Copy), reads=[ps.b], writes=[gbf.b])
                        for c0 in range(0, N, 512):
                            n = min(512, N - c0)
                            ps = self.psum()
                            for kc in range(16):
                                S.op("pe", MM(ps[:, :n], v_[:, kc, :], xw_[:, kc, halo + c0:halo + c0 + n], kc == 0, kc == 15), reads=[v_.b, xw_.b], writes=[ps.b], inc=(kc == 15))
                            S.op("act", ACT(vbf[:, c0:c0 + n], ps[:, :n], AF.Copy), reads=[ps.b], writes=[vbf.b])
                        eng = "dve"
                        rd = [gbf.b, cwt.b, cbt.b, a.b]
                        R = N // GW
                        gv = gbf[:, :].rearrange("p (r c) -> p r c", c=GW)
                        av = a[:, :].rearrange("p (r c) -> p r c", c=GW)
                        r0 = 0 if isctx else 1
                        S.op(eng, TS(av, gv[:, r0:r0 + R, :], cwt[:, f, 4:5], cbt[:, f:f + 1], ALU.mult, ALU.add), reads=rd, writes=[a.b])
                        for dr_ in ((0,) if isctx else (-1, 0, 1)):
                            for dc in (-1, 0, 1):
                                if dr_ == 0 and dc == 0:
                                    continue
                                tap = cwt[:, f, (dr_ + 1) * 3 + dc + 1:(dr_ + 1) * 3 + dc + 2]
                                o0, o1 = max(0, -dc), GW - max(0, dc)
                                S.op(eng, STT(av[:, :, o0:o1], gv[:, r0 + dr_:r0 + dr_ + R, o0 + dc:o1 + dc], tap, av[:, :, o0:o1], ALU.mult, ALU.add),
                                     reads=rd, writes=[a.b])
                        S.op("act", ACT(a[:], a[:], AF.Silu), reads=[a.b], writes=[a.b])
                        S.op(eng, TT(actb[:, f, :], a[:], vbf[:], ALU.mult), reads=[a.b, vbf.b], writes=[actb.b])
                with self.scope() as es2:
                    wd = [self.sb(es2, "wd%d" % i, [128, FT, 128], BF16) for i in range(2)]
                    hs = [self.sb(es2, "hs2%d" % i, [128, 512]) for i in range(2)]
                    it = 0
                    for j in range(16):
                        w = wd[j % 2]
                        self.wtile(w, wdn, j * 128, FT)
                        for c0 in range(0, N, 512):
                            n = min(512, N - c0)
                            h = hs[it % 2]
                            it += 1
                            S.dma("sp", h[:, :n], h1_d[j * 128:(j + 1) * 128, oc0 + c0:oc0 + c0 + n], reads=[h1_d.b], writes=[h.b], stream="lda")
                            ps = self.psum()
                            for kc in range(FT):
                                S.op("pe", MM(ps[:, :n], w[:, kc, :], actb[:, kc, c0:c0 + n], kc == 0, kc == FT - 1), reads=[w.b, actb.b], writes=[ps.b], inc=(kc == FT - 1))
                            S.op("act", ACT(h[:, :n], h[:, :n], AF.Copy, scale=ALPHA), reads=[h.b], writes=[h.b])
                            S.op("dve", STT(h[:, :n], ps[:, :n], self.modv(5, j, jm), h[:, :n], ALU.mult, ALU.add), reads=[ps.b, h.b, self.mod.b], writes=[h.b])
                            S.dma("sp", pre2[j * 128:(j + 1) * 128, oc0 + c0:oc0 + c0 + n], h[:, :n], reads=[h.b], writes=[pre2.b], stream="st")
            with self.scope() as es:
                lnb = self.ln_alloc(es)
                for c0 in range(0, N, 512):
                    n = min(512, N - c0)
                    if isctx:
                        dst = [(dr["hctx_d"], c0, "h")]
                    elif last:
                        dst = [(self.out, oc0 - TC + c0, "h")]
                    else:
                        dst = [(dr["hlat_d"], oc0 - TC + c0, "h")]
                    self.layernorm(lnb, l, 2, pre2, slice(oc0 + c0, oc0 + c0 + n), n, dst)
        if not last:
            with self.scope() as es:
                hh = self.sb(es, "hh", [128, 16, 2])
                hl = dr["hlat_d"]
                S.dma("sp", hh[:, :, 0:1], hl[:, 0:1].rearrange("(kc p) t -> p kc t", p=128), reads=[hl.b], writes=[hh.b], stream="lda", slow=True)
                S.dma("sp", hh[:, :, 1:2], hl[:, TL - 1:TL].rearrange("(kc p) t -> p kc t", p=128), reads=[hl.b], writes=[hh.b], stream="lda", slow=True)
                S.dma("sp", dr["hhloc"][:, :].rearrange("(kc p) t -> p kc t", p=128), hh[:], reads=[hh.b], writes=[dr["hhloc"].b], stream="st", slow=True)
                S.allgather(dr["hhloc"].t, dr["hhfull"].t, [dr["hhloc"].b], [dr["hhfull"].b], self.dummy[:])
                ar_ = self.sb(es, "hhall", [128, 8, 16, 2])
                S.dma("sp", ar_[:].rearrange("p r k t -> p (r k) t"), dr["hhfull"][:, :].rearrange("(rk p) t -> p rk t", p=128),
                      reads=[dr["hhfull"].b], writes=[ar_.b], stream="lda", slow=True)
                ac = self.sb(es, "hhacc", [128, 16, 2])
                S.op("dve", lambda e: e.memset(ac[:], 0.0), writes=[ac.b])
                for r in range(8):
                    S.op("dve", STT(ac[:, :, 0:1], ar_[:, r, :, 1:2], self.rank[:, 8 + r:9 + r], ac[:, :, 0:1], ALU.mult, ALU.add),
                         reads=[ar_.b, self.rank.b, ac.b], writes=[ac.b])
                    S.op("dve", STT(ac[:, :, 1:2], ar_[:, r, :, 0:1], self.rank[:, 16 + r:17 + r], ac[:, :, 1:2], ALU.mult, ALU.add),
                         reads=[ar_.b, self.rank.b, ac.b], writes=[ac.b])
                S.dma("sp", dr["hhalo_d"][:, :].rearrange("(kc p) t -> p kc t", p=128), ac[:], reads=[ac.b], writes=[dr["hhalo_d"].b], stream="st", slow=True)


def _f32(a):
    return np.ascontiguousarray(np.asarray(a, dtype=np.float32))


def host_inputs(inp):
    g = lambda k: np.asarray(inp[k], np.float32)
    x = g("x")[0]
    ctx = g("ctx")[0]
    c = g("c").reshape(D)
    c_ctx = g("c_ctx").reshape(D)
    L = DEPTH
    sh = {}
    sh["ctx_fm"] = _f32(ctx.T)
    sh["cc"] = _f32(np.concatenate([c.reshape(16, 128).T, c_ctx.reshape(16, 128).T], axis=1))
    sh["ident"] = np.eye(128, dtype=np.float32)
    s_i = np.arange(128) // 16
    sh["masks"] = _f32(np.stack([(s_i[None, :] >= s_i[:, None]), (s_i[:, None] >= s_i[None, :])], axis=1))
    ar = np.arange(128)
    sh["tri"] = _f32(np.stack([(ar[:, None] <= ar[None, :]), (ar[:, None] >= ar[None, :])], axis=1))
    es = np.zeros((8, 2, 128), np.float32)
    for s_ in range(8):
        es[7 - s_, 0, s_ * 16:(s_ + 1) * 16] = 1.0
        es[s_, 1, s_ * 16:(s_ + 1) * 16] = 1.0
    sh["esel"] = es
    sh["ssd_conv_w"] = _f32(g("ssd_conv_w").reshape(L, 3, 24, 128).transpose(3, 0, 2, 1))
    sh["ssd_conv_b"] = _f32(g("ssd_conv_b").reshape(L, 24, 128).transpose(2, 0, 1))
    dtb = g("ssd_dt_bias").reshape(L, 64)
    alog = g("ssd_a_log").reshape(L, 64)
    kk = (np.arange(64) // 32).astype(np.float32)
    da = np.stack([dtb, alog, np.broadcast_to(1.0 - kk, (L, 64)), np.broadcast_to(kk, (L, 64))], axis=-1)
    sh["dtb_alog"] = _f32(np.concatenate([da, da], axis=1).transpose(1, 0, 2))
    chan = np.arange(2048).reshape(16, 128)
    sh["ssd_dvec"] = _f32(g("ssd_d")[:, chan // 64].transpose(2, 0, 1))
    sh["ssd_nw"] = _f32(g("ssd_norm_w").reshape(L, 16, 128).transpose(2, 0, 1))
    lam = np.stack([g("s5_lam_re"), g("s5_lam_im"), np.broadcast_to(g("s5_log_step")[..., None], (L, 2, 64, 64))], axis=1)
    lam = lam.reshape(L, 3, 2, 32, 2, 64)
    sh["s5_lamc"] = _f32(lam.transpose(4, 5, 0, 1, 2, 3).reshape(128, L, 3, 64))
    cc_ = np.stack([g("s5_c_re"), g("s5_c_im")], axis=0)
    cc_ = cc_.reshape(2, L, 2, 32, 2, 16, 64)
    sh["s5_cc"] = _f32(cc_.transpose(4, 6, 1, 2, 3, 0, 5).reshape(128, L, 2, 32, 2, 16))
    bb = np.stack([g("s5_b_re"), g("s5_b_im")], axis=0)
    sh["s5_bt"] = _f32(bb.transpose(4, 1, 0, 2, 3).reshape(16, L, 2, 4096))
    sd = g("s5_d").reshape(L, 64, 16)
    sh["s5_dcol"] = _f32(np.tile(sd.transpose(2, 0, 1)[None], (8, 1, 1, 1)).reshape(128, L, 64))
    lg = np.stack([g("ln1_g"), g("ln1_b"), g("ln2_g"), g("ln2_b")], axis=1)
    sh["ln_gb"] = _f32(lg.reshape(L, 4, 16, 128).transpose(3, 0, 1, 2))
    sh["ffn_cw"] = _f32(g("ffn_conv_w").reshape(L, 9, FT, 128).transpose(3, 0, 2, 1))
    sh["ffn_cb"] = _f32(g("ffn_conv_b").reshape(L, FT, 128).transpose(2, 0, 1))
    wnames = {"w_in": "w_in", "w_glu": "s5_w_glu", "w_s5p": "s5_w_proj", "w_ssdp": "ssd_w_proj", "w_out": "w_out",
              "w_up": "w_up", "w_down": "w_down"}
    wada = g("w_ada")
    bada = g("b_ada").reshape(L, 96, 128)
    maps = []
    for r in range(NCORES):
        m = dict(sh)
        m["x_fm"] = _f32(x[r * TL:(r + 1) * TL].T)
        xh = np.zeros((D, 2), np.float32)
        if r > 0:
            xh[:, 0] = x[r * TL - 1]
        if r < NCORES - 1:
            xh[:, 1] = x[(r + 1) * TL]
        m["xhalo"] = xh
        rk = np.zeros((128, 32), np.float32)
        rk[:, r] = 1.0
        if r > 0:
            rk[:, 8 + r - 1] = 1.0
            rk[:, 24] = 1.0
        if r < NCORES - 1:
            rk[:, 16 + r + 1] = 1.0
            rk[:, 25] = 1.0
        m["rank"] = rk
        m["w_ada_sl"] = _f32(wada[:, :, r * 1536:(r + 1) * 1536])
        m["b_ada_sl"] = _f32(bada[:, r * 12:(r + 1) * 12, :].transpose(2, 0, 1))
        for short, full in wnames.items():
            w = g(full)
            rows = w.shape[1] // NCORES
            for l in range(L):
                m["%s%d_sh" % (short, l)] = _f32(w[l, r * rows:(r + 1) * rows, :])
        maps.append(m)
    return maps


_CACHE = {}


def kernel(**inputs):
    maps = host_inputs(inputs)
    if "nc" not in _CACHE:
        nc = bass.Bass("TRN2", target_bir_lowering=False)
        P = Prog(nc)
        P.build()
        _CACHE["nc"] = nc
        _CACHE["names"] = set(k for k, v in P.dram.items())
    nc = _CACHE["nc"]
    names = _CACHE["names"]
    maps = [{k: v for k, v in m.items() if k in names} for m in maps]
    res = run_bass_kernel_spmd(nc, maps, core_ids=list(range(NCORES)))
    out = np.empty((1, SEQ, D), np.float32)
    for r in range(NCORES):
        out[0, r * TL:(r + 1) * TL, :] = np.asarray(res.results[r]["y_fm"], np.float32).T
    return out
```
